# Optimizing a Trainium2 kernel written in Bass

```python
import jax, jax.numpy as jnp
from jax import lax
import numpy as np

D_MODEL = 2048
BATCH = 4
SEQ = 2048
DEPTH = 2

N_META = 16
GRID_W = 64
GLA_HEADS = 4
GLA_DK = 128
GLA_DV = 256
GLA_RANK = 16
GLA_TAU = 16.0
GLA_CHUNK = 64
NA_HEADS = 16
NA_DH = 64
NA_WIN_H = 8
NA_WIN_W = 16
NA_QB = 16
NA_KB = 32
D_FF = 4 * D_MODEL
DEEPNORM_ALPHA = (2 * DEPTH) ** 0.25
DEEPNORM_BETA = (8 * DEPTH) ** -0.25
LN_EPS = 1e-5

MIX_WIDTH = GLA_HEADS * GLA_DV + NA_HEADS * NA_DH
PROJ_SIZES = (GLA_HEADS * GLA_DK, GLA_HEADS * GLA_DK, GLA_HEADS * GLA_DV, GLA_HEADS * GLA_DV,
              2 * GLA_RANK, NA_HEADS * NA_DH, NA_HEADS * NA_DH, NA_HEADS * NA_DH)

kernel_name = "hybrid_gla_natten_deepnorm_encoder"


def _layer_norm(x, w, b):
    xf = x.astype(jnp.float32)
    mu = jnp.mean(xf, axis=-1, keepdims=True)
    var = jnp.mean(jnp.square(xf - mu), axis=-1, keepdims=True)
    y = (xf - mu) * lax.rsqrt(var + LN_EPS) * w.astype(jnp.float32) + b.astype(jnp.float32)
    return y.astype(x.dtype)


def _to_chunks(t, n_heads):
    B, Lp, W = t.shape
    return t.reshape(B, Lp // GLA_CHUNK, GLA_CHUNK, n_heads, W // n_heads).transpose(1, 0, 3, 2, 4)


def _from_chunks(t):
    N, B, H, C, d = t.shape
    return t.transpose(1, 0, 3, 2, 4).reshape(B, N * C, H * d)


def _gla_scan(q, k, v, g):
    qc, kc, vc, gc = (_to_chunks(t, GLA_HEADS) for t in (q, k, v, g))
    B = q.shape[0]
    C = GLA_CHUNK
    incl = jnp.tril(jnp.ones((C, C), dtype=bool))[:, :, None]

    def step(S, inp):
        qi, ki, vi, gi = inp
        bcum = jnp.cumsum(gi, axis=-2)
        b_last = bcum[..., -1:, :]
        inter = jnp.einsum('bhtd,bhde->bhte', qi * jnp.exp(bcum), S)
        diff = bcum[..., :, None, :] - bcum[..., None, :, :]
        decay = jnp.exp(jnp.where(incl, diff, -jnp.inf))
        scores = jnp.einsum('bhtd,bhsd,bhtsd->bhts', qi, ki, decay)
        intra = jnp.einsum('bhts,bhse->bhte', scores, vi)
        S = (jnp.exp(b_last[..., 0, :])[..., None] * S
             + jnp.einsum('bhsd,bhse->bhde', ki * jnp.exp(b_last - bcum), vi))
        return S, inter + intra

    S0 = jnp.zeros((B, GLA_HEADS, GLA_DK, GLA_DV), jnp.float32)
    _, out = lax.scan(step, S0, (qc, kc, vc, gc))
    return _from_chunks(out)


def _gla_mixer(q, k, v, r, gate_lr, w_up, b_up, norm_w):
    B, L, _ = q.shape
    f32 = jnp.float32
    pad = GLA_CHUNK - N_META
    lr = gate_lr.astype(f32).reshape(B, L, 2, GLA_RANK)
    g = jax.nn.log_sigmoid(jnp.einsum('blzr,zrk->blzk', lr, w_up.astype(f32))
                           + b_up.astype(f32)) / GLA_TAU
    padf = lambda t: jnp.pad(t, ((0, 0), (pad, 0), (0, 0)))
    qp = padf(q.astype(f32) * GLA_DK ** -0.5)
    kp = padf(k.astype(f32))
    vp = padf(v.astype(f32))
    g_fwd = padf(g[:, :, 0])
    g_bwd = padf(g[:, :, 1])
    flip = lambda t: jnp.flip(t, axis=1)
    o_fwd = _gla_scan(qp, kp, vp, g_fwd)
    o_bwd = flip(_gla_scan(flip(qp), flip(kp), flip(vp), flip(g_bwd)))
    o = (o_fwd + o_bwd)[:, pad:].reshape(B, L, GLA_HEADS, GLA_DV)
    o = o * lax.rsqrt(jnp.mean(jnp.square(o), axis=-1, keepdims=True) + LN_EPS) * norm_w.astype(f32)
    o = o.reshape(B, L, GLA_HEADS * GLA_DV) * jax.nn.silu(r.astype(f32))
    return o.astype(q.dtype)


def _na_mixer(q, k, v, rel_bias):
    B, L, _ = q.shape
    T = L - N_META
    rows = T // GRID_W
    kh = min(NA_WIN_H, rows)
    ncb = GRID_W // NA_QB
    f32 = jnp.float32
    split = lambda t: t.astype(f32).reshape(B, L, NA_HEADS, NA_DH).transpose(0, 2, 1, 3)
    q, k, v = split(q) * NA_DH ** -0.5, split(k), split(v)
    qm, km, vm = q[:, :, :N_META], k[:, :, :N_META], v[:, :, :N_META]
    meta_out = jnp.einsum('bhqk,bhkd->bhqd',
                          jax.nn.softmax(jnp.einsum('bhqd,bhkd->bhqk', qm, km), axis=-1), vm)
    qg = q[:, :, N_META:].reshape(B, NA_HEADS, rows, GRID_W, NA_DH)
    kg = k[:, :, N_META:].reshape(B, NA_HEADS, rows, GRID_W, NA_DH)
    vg = v[:, :, N_META:].reshape(B, NA_HEADS, rows, GRID_W, NA_DH)

    col = np.arange(GRID_W)
    cs = np.clip(col - NA_WIN_W // 2, 0, GRID_W - NA_WIN_W).reshape(ncb, NA_QB)
    q_col = col.reshape(ncb, NA_QB)
    cb = np.clip(np.arange(ncb) * NA_QB - NA_WIN_W // 2, 0, GRID_W - NA_KB)
    key_col = cb[:, None] + np.arange(NA_KB)
    col_mask = ((key_col[:, None, :] >= cs[:, :, None])
                & (key_col[:, None, :] < cs[:, :, None] + NA_WIN_W))
    dc_idx = np.clip(key_col[:, None, :] - q_col[:, :, None],
                     -(NA_WIN_W - 1), NA_WIN_W - 1) + NA_WIN_W - 1
    bias_tab = rel_bias.astype(f32)
    n_loc = kh * NA_KB

    def row_attn(args):
        r, q_row = args
        rs = jnp.clip(r - kh // 2, 0, rows - kh)
        k_blk = lax.dynamic_slice_in_dim(kg, rs, kh, axis=2)[:, :, :, key_col, :]
        v_blk = lax.dynamic_slice_in_dim(vg, rs, kh, axis=2)[:, :, :, key_col, :]
        qb = q_row.reshape(B, NA_HEADS, ncb, NA_QB, NA_DH)
        s_loc = jnp.einsum('bhnqd,bhrnkd->bhnqrk', qb, k_blk)
        dr_idx = rs + jnp.arange(kh) - r + NA_WIN_H - 1
        bias = bias_tab[:, dr_idx[:, None, None, None], dc_idx[None]]
        bias = jnp.transpose(bias, (0, 2, 3, 1, 4))
        s_loc = jnp.where(col_mask[:, :, None, :], s_loc + bias, -jnp.inf)
        s_meta = jnp.einsum('bhnqd,bhmd->bhnqm', qb, km)
        s = jnp.concatenate([s_loc.reshape(B, NA_HEADS, ncb, NA_QB, n_loc), s_meta], axis=-1)
        p = jax.nn.softmax(s, axis=-1)
        p_loc = p[..., :n_loc].reshape(B, NA_HEADS, ncb, NA_QB, kh, NA_KB)
        out = (jnp.einsum('bhnqrk,bhrnkd->bhnqd', p_loc, v_blk)
               + jnp.einsum('bhnqm,bhmd->bhnqd', p[..., n_loc:], vm))
        return out.reshape(B, NA_HEADS, GRID_W, NA_DH)

    grid_out = lax.map(row_attn, (jnp.arange(rows), jnp.moveaxis(qg, 2, 0)))
    grid_out = jnp.moveaxis(grid_out, 0, 2).reshape(B, NA_HEADS, T, NA_DH)
    out = jnp.concatenate([meta_out, grid_out], axis=2)
    return out.transpose(0, 2, 1, 3).reshape(B, L, NA_HEADS * NA_DH)


def _layer(h, w_in, gla_w_up, gla_b_up, gla_norm_w, na_rel_bias, w_out,
           ln1_w, ln1_b, w_ff1, w_ff2, ln2_w, ln2_b):
    proj = jnp.einsum('bld,dp->blp', h, w_in)
    points = [int(p) for p in np.cumsum(PROJ_SIZES)[:-1]]
    gq, gk, gv, gr, glr, nq, nk, nv = jnp.split(proj, points, axis=-1)
    y_gla = _gla_mixer(gq, gk, gv, gr, glr, gla_w_up, gla_b_up, gla_norm_w)
    y_na = _na_mixer(nq, nk, nv, na_rel_bias).astype(h.dtype)
    mix = jnp.einsum('blm,md->bld', jnp.concatenate([y_gla, y_na], axis=-1), w_out)
    h = _layer_norm(DEEPNORM_ALPHA * h + mix, ln1_w, ln1_b)
    ff = jnp.einsum('blf,fd->bld', jnp.square(jax.nn.relu(jnp.einsum('bld,df->blf', h, w_ff1))), w_ff2)
    return _layer_norm(DEEPNORM_ALPHA * h + ff, ln2_w, ln2_b)


def setup_inputs(seed: int = 0) -> dict:
    key = jax.random.key(seed)
    ks = jax.random.split(key, 14)
    nrm = jax.random.normal
    f32 = jnp.float32
    p_total = sum(PROJ_SIZES)
    return {
        "x": nrm(ks[0], (BATCH, SEQ, D_MODEL), f32),
        "meta": nrm(ks[1], (N_META, D_MODEL), f32),
        "w_in": nrm(ks[2], (DEPTH, D_MODEL, p_total), f32) * D_MODEL ** -0.5,
        "gla_w_up": nrm(ks[3], (DEPTH, 2, GLA_RANK, GLA_HEADS * GLA_DK), f32) * GLA_RANK ** -0.5,
        "gla_b_up": 0.1 * nrm(ks[4], (DEPTH, 2, GLA_HEADS * GLA_DK), f32),
        "gla_norm_w": 1.0 + 0.02 * nrm(ks[5], (DEPTH, GLA_DV), f32),
        "na_rel_bias": 0.1 * nrm(ks[6], (DEPTH, NA_HEADS, 2 * NA_WIN_H - 1, 2 * NA_WIN_W - 1), f32),
        "w_out": nrm(ks[7], (DEPTH, MIX_WIDTH, D_MODEL), f32) * (MIX_WIDTH ** -0.5 * DEEPNORM_BETA),
        "ln1_w": 1.0 + 0.02 * nrm(ks[8], (DEPTH, D_MODEL), f32),
        "ln1_b": 0.02 * nrm(ks[9], (DEPTH, D_MODEL), f32),
        "w_ff1": nrm(ks[10], (DEPTH, D_MODEL, D_FF), f32) * D_MODEL ** -0.5,
        "w_ff2": nrm(ks[11], (DEPTH, D_FF, D_MODEL), f32) * (D_FF ** -0.5 * DEEPNORM_BETA),
        "ln2_w": 1.0 + 0.02 * nrm(ks[12], (DEPTH, D_MODEL), f32),
        "ln2_b": 0.02 * nrm(ks[13], (DEPTH, D_MODEL), f32),
    }


def reference(x, meta, w_in, gla_w_up, gla_b_up, gla_norm_w, na_rel_bias, w_out,
              ln1_w, ln1_b, w_ff1, w_ff2, ln2_w, ln2_b):
    B = x.shape[0]
    meta_b = jnp.broadcast_to(meta.astype(x.dtype)[None], (B, N_META, x.shape[-1]))
    h = jnp.concatenate([meta_b, x], axis=1)
    for l in range(DEPTH):
        h = _layer(h, w_in[l], gla_w_up[l], gla_b_up[l], gla_norm_w[l], na_rel_bias[l], w_out[l],
                   ln1_w[l], ln1_b[l], w_ff1[l], w_ff2[l], ln2_w[l], ln2_b[l])
    return h[:, N_META:]
```

```python
import contextlib
import numpy as np
import concourse.bass as bass
import concourse.mybir as mybir
from concourse.bass_utils import run_bass_kernel_spmd

F32 = mybir.dt.float32
BF16 = mybir.dt.bfloat16
AF = mybir.ActivationFunctionType
ALU = mybir.AluOpType

D = 2048
LP = 2112
NCH = 33
KC = 16
DFF = 8192
PT = 6176
C_GQ, C_GK, C_GV, C_GR, C_LR, C_NQ, C_NK, C_NV = 0, 512, 1024, 2048, 3072, 3104, 4128, 5152
ALPHA = 4.0 ** 0.25
EPS = 1e-5
DEPTH = 2
NCORES = 4

ENGS = ("tensor", "vector", "scalar", "gpsimd", "sync")


class Prog:
    def __init__(self, nc, stack, n_dma=20, sync_same=True):
        self.nc = nc
        self.n_dma = n_dma
        self.sync_same = sync_same
        self.esem = {e: stack.enter_context(nc.semaphore("s_" + e)) for e in ENGS}
        self.dq = ("sync", "gpsimd")
        self.dsem = {q: [stack.enter_context(nc.semaphore(f"d_{q}_{k}")) for k in range(n_dma)] for q in self.dq}
        self.cnt = {e: 0 for e in ENGS}
        self.dma_k = {q: 0 for q in self.dq}
        self.dval = {q: [0] * n_dma for q in self.dq}
        self.ops = []
        self.nops = 0

    def add(self, eng, fn, reads=(), writes=(), dma=False):
        self.ops.append((eng, fn, tuple(reads), tuple(writes), dma))

    def dma(self, out, in_, r, w, q="sync"):
        self.add(q, lambda e: e.dma_start(out=out, in_=in_), r, w, dma=True)

    def mm(self, out, lhsT, rhs, start, stop, r, w):
        self.add("tensor", lambda e: e.matmul(out, lhsT, rhs, start=start, stop=stop), r, w)

    def tr(self, out, in_, ident, r, w):
        self.add("tensor", lambda e: e.transpose(out, in_, ident), r, w)

    def act(self, out, in_, func, r, w, bias=None, scale=None, accum_out=None):
        kw = {}
        if bias is not None:
            kw["bias"] = bias
        if scale is not None:
            kw["scale"] = scale
        if accum_out is not None:
            kw["accum_out"] = accum_out
        self.add("scalar", lambda e: e.activation(out, in_, func, **kw), r, w)

    def copy(self, eng, out, in_, r, w):
        if eng == "scalar":
            self.add(eng, lambda e: e.copy(out, in_), r, w)
        else:
            self.add(eng, lambda e: e.tensor_copy(out, in_), r, w)

    def tt(self, eng, out, in0, in1, op, r, w):
        self.add(eng, lambda e: e.tensor_tensor(out, in0, in1, op), r, w)

    def ts(self, eng, out, in0, s1, op0, r, w, s2=None, op1=None):
        if op1 is None:
            self.add(eng, lambda e: e.tensor_scalar(out, in0, s1, None, op0), r, w)
        else:
            self.add(eng, lambda e: e.tensor_scalar(out, in0, s1, s2, op0, op1), r, w)

    def stt(self, eng, out, in0, scalar, in1, op0, op1, r, w):
        self.add(eng, lambda e: e.scalar_tensor_tensor(out, in0, scalar, in1, op0, op1), r, w)

    def memset(self, eng, ap, val, r, w):
        self.add(eng, lambda e: e.memset(ap, val), r, w)

    def emit(self):
        nc = self.nc
        ops = self.ops
        self.ops = []
        n = len(ops)
        self.nops += n
        last_w = {}
        readers = {}
        deps = [None] * n
        for i, (eng, fn, reads, writes, dma) in enumerate(ops):
            d = set()
            for r in reads:
                j = last_w.get(r)
                if j is not None:
                    d.add(j)
            for w in writes:
                j = last_w.get(w)
                if j is not None:
                    d.add(j)
                rr = readers.get(w)
                if rr:
                    d.update(rr)
            d.discard(i)
            for r in reads:
                readers.setdefault(r, []).append(i)
            for w in writes:
                last_w[w] = i
                readers[w] = []
            deps[i] = d
        signal = [False] * n
        need = [None] * n
        for i in range(n):
            eng_i = ops[i][0]
            dma_i = ops[i][4]
            lst = []
            for j in deps[i]:
                eng_j = ops[j][0]
                dma_j = ops[j][4]
                if not dma_j and not dma_i and eng_j == eng_i:
                    if eng_i == "tensor" or not self.sync_same:
                        continue
                lst.append(j)
                signal[j] = True
            need[i] = lst
        last_on = {}
        for i in range(n):
            if not ops[i][4]:
                last_on[ops[i][0]] = i
        for e, i in last_on.items():
            signal[i] = True
        count = [0] * n
        dslot = [None] * n
        for i in range(n):
            eng, fn, reads, writes, dma = ops[i]
            if dma:
                k = self.dma_k[eng]
                self.dma_k[eng] += 1
                s = k % self.n_dma
                self.dval[eng][s] += 16
                dslot[i] = (eng, s, self.dval[eng][s])
            elif signal[i]:
                self.cnt[eng] += 1
                count[i] = self.cnt[eng]
        fin_cnt = dict(self.cnt)
        fin_dval = {q: list(v) for q, v in self.dval.items()}
        esem, dsem = self.esem, self.dsem

        def make(engname):
            def body(eng):
                waited = {}

                def do_wait(key, sem, val):
                    if waited.get(key, 0) >= val:
                        return
                    waited[key] = val
                    eng.wait_ge(sem, val)

                for i in range(n):
                    e_i, fn, reads, writes, dma = ops[i]
                    if e_i != engname:
                        continue
                    waits = {}
                    for j in need[i]:
                        if ops[j][4]:
                            qe, s, val = dslot[j]
                            key = ("d", qe, s)
                            sem = dsem[qe][s]
                        else:
                            key = ("e", ops[j][0])
                            sem = esem[ops[j][0]]
                            val = count[j]
                        if waits.get(key, (None, 0))[1] < val:
                            waits[key] = (sem, val)
                    if dma:
                        qe, s, val = dslot[i]
                        if val > 16:
                            key = ("d", qe, s)
                            if waits.get(key, (None, 0))[1] < val - 16:
                                waits[key] = (dsem[qe][s], val - 16)
                    for key, (sem, val) in waits.items():
                        do_wait(key, sem, val)
                    ins = fn(eng)
                    if dma:
                        qe, s, val = dslot[i]
                        ins.then_inc(dsem[qe][s], 16)
                    elif signal[i]:
                        ins.then_inc(esem[engname], 1)
                for e in ENGS:
                    if fin_cnt[e] > 0:
                        do_wait(("e", e), esem[e], fin_cnt[e])
                for q in self.dq:
                    for s in range(self.n_dma):
                        if fin_dval[q][s] > 0:
                            do_wait(("d", q, s), dsem[q][s], fin_dval[q][s])

            return body

        with nc.Block() as block:
            block.tensor(make("tensor"))
            block.vector(make("vector"))
            block.scalar(make("scalar"))
            block.gpsimd(make("gpsimd"))
            block.sync(make("sync"))


class Ring:
    def __init__(self, tiles, name):
        self.tiles = tiles
        self.name = name
        self.i = -1

    def next(self):
        self.i += 1
        k = self.i % len(self.tiles)
        return self.tiles[k], (self.name, k)


class Ctx:
    pass


def _alloc(nc, st, pfx=""):
    def sb(name, shape, dt):
        return st.enter_context(nc.sbuf_tensor(pfx + name, list(shape), dt))

    def ps(name, shape, dt=F32):
        return st.enter_context(nc.psum_tensor(pfx + name, list(shape), dt))

    def ring(name, n, shape, dt):
        return Ring([sb(f"{name}{k}", shape, dt) for k in range(n)], name)

    def pring(name, n, shape, dt=F32):
        return Ring([ps(f"{name}{k}", shape, dt) for k in range(n)], name)

    return sb, ps, ring, pring


def conv_weight(P, dst, src, rows, name, step=512):
    for r0 in range(0, rows, step):
        P.dma(dst[r0:r0 + step, :], src[r0:r0 + step, :], [], [(name, r0)], q="gpsimd")


def phase0(nc, P, T):
    import os
    MODE = os.environ.get("DBG0", "all")
    with contextlib.ExitStack() as st:
        sb, ps, ring, pring = _alloc(nc, st, "p0_")
        zero = sb("zero", [48, D], F32)
        idb = sb("idb", [128, 128], BF16)
        P.memset("gpsimd", zero[:, :], 0.0, [], ["zero"])
        P.dma(T.Hres[0:48, :], zero[:, :], ["zero"], ["HresP"], q="gpsimd")
        P.dma(T.Hres[48:64, :], T.meta, [], ["HresM"], q="gpsimd")
        for j in range(4):
            P.dma(T.Hres[64 + 512 * j:64 + 512 * (j + 1), :], T.x[512 * j:512 * (j + 1), :], [], [("HresX", j)], q="gpsimd")
        P.dma(idb[:, :], T.ident, [], ["idb"], q="gpsimd")
        if MODE in ("all", "conv"):
            conv_weight(P, T.w_in_b[0], T.w_in[0], D, "cw_in0")
        if MODE in ("conv", "init"):
            P.emit()
            return
        x32 = ring("x32", 2, [128, D], F32)
        xb = ring("xb", 2, [128, D], BF16)
        pT = pring("pT", 2, [128, KC, 128], BF16)
        stg = ring("stg", 2, [128, KC, 128], BF16)
        tiles = [int(v) for v in os.environ.get("DBGT", ",".join(str(v) for v in range(17))).split(",")]
        for i in tiles:
            M = 128 if i < 16 else 64
            xt, xr = x32.next()
            if i == 0:
                P.memset("vector", xt[0:64, :], 0.0, [], [xr])
                P.dma(xt[48:64, :], T.meta, [xr], [(xr, "m")])
                P.dma(xt[64:128, :], T.x[0:64, :], [], [(xr, "x")])
                rd = [xr, (xr, "m"), (xr, "x")]
            else:
                P.dma(xt[0:M, :], T.x[128 * i - 64:128 * i - 64 + M, :], [], [xr, (xr, "m"), (xr, "x")])
                rd = [xr]
            xbt, xbr = xb.next()
            P.copy("scalar", xbt[0:M, :], xt[0:M, :], rd, [xbr])
            pt, pr = pT.next()
            for kc in range(KC):
                P.tr(pt[:, kc, 0:M], xbt[0:M, 128 * kc:128 * (kc + 1)], idb[0:M, 0:M], [xbr, "idb"], [pr])
            sg, sr = stg.next()
            P.copy("vector", sg[:, :, 0:M], pt[:, :, 0:M], [pr], [sr])
            P.dma(T.XT_d[:, :, 128 * i:128 * i + M], sg[:, :, 0:M], [sr], [("XTd", i)])
        P.emit()


def phaseA(nc, P, T, l):
    with contextlib.ExitStack() as st:
        sb, ps, ring, pring = _alloc(nc, st, f"A{l}_")
        XT = sb("XT", [128, KC, LP], BF16)
        for g in range(4):
            P.dma(XT[:, 4 * g:4 * g + 4, :], T.XT_d[:, 4 * g:4 * g + 4, :], [], [("XT", g)])
        conv_weight(P, T.w_out_b[l], T.w_out[l], D, "cw_out")
        if l < T.ffl:
            conv_weight(P, T.w_ff1_b[l], T.w_ff1[l], D, "cw_ff1", step=256)
        WinV = T.w_in_b[l].rearrange("(kc k) c -> k kc c", k=128)
        psr = pring("psA", 4, [128, 512])
        XTR = [("XT", g) for g in range(4)]
        TG = [(0, 512), (512, 512), (1024, 512), (1536, 512), (2048, 64)]
        wfm = ring("wfm", 3, [128, KC, 128], BF16)
        stg = ring("stgfm", 2, [128, LP], BF16)
        fm = [(C_GQ + 128 * h, h, 128.0 ** -0.5) for h in range(4)]
        fm += [(C_GK + 128 * h, 4 + h, 1.0) for h in range(4)]
        fm += [(C_NQ + 128 * j, 8 + j, 0.125) for j in range(8)]
        fm += [(C_NK + 128 * j, 16 + j, 1.0) for j in range(8)]
        ev = 0
        for (c0, blk, scale) in fm:
            wt, wr = wfm.next()
            P.dma(wt[:, :, :], WinV[:, :, c0:c0 + 128], [], [wr])
            sg, sr = stg.next()
            for tg, (t0, n) in enumerate(TG):
                pt, pr = psr.next()
                for kc in range(KC):
                    P.mm(pt[:, 0:n], wt[:, kc, :], XT[:, kc, t0:t0 + n], kc == 0, kc == KC - 1, [wr, ("XT", kc // 4)], [pr])
                if ev % 2 == 0:
                    P.act(sg[:, t0:t0 + n], pt[:, 0:n], AF.Identity, [pr], [(sr, tg)], scale=scale)
                else:
                    P.ts("vector", sg[:, t0:t0 + n], pt[:, 0:n], scale, ALU.mult, [pr], [(sr, tg)])
                ev += 1
            P.dma(T.QK_d[blk], sg[:, :], [(sr, tg) for tg in range(5)], [("QK_d", blk)])
        wlr = sb("wlr", [128, KC, 32], BF16)
        P.dma(wlr[:, :, :], WinV[:, :, C_LR:C_LR + 32], [], ["wlr"])
        slr = ring("slr", 2, [16, LP], F32)
        for z in range(2):
            sg, sr = slr.next()
            for tg, (t0, n) in enumerate(TG):
                pt, pr = psr.next()
                for kc in range(KC):
                    P.mm(pt[0:16, 0:n], wlr[:, kc, 16 * z:16 * z + 16], XT[:, kc, t0:t0 + n], kc == 0, kc == KC - 1, ["wlr", ("XT", kc // 4)], [pr])
                P.copy("vector", sg[:, t0:t0 + n], pt[0:16, 0:n], [pr], [(sr, tg)])
            P.dma(T.LRT_d[z], sg[:, :], [(sr, tg) for tg in range(5)], [("LRT_d", z)])
        wtm = ring("wtm", 2, [128, KC, 512], BF16)
        sb16 = ring("stb", 3, [128, 512], BF16)
        sf32 = ring("stf", 3, [128, 512], F32)
        tmb = [(C_GK, T.TMK, 0, BF16), (C_GV, T.TMV, 0, BF16), (C_GV + 512, T.TMV, 512, BF16),
               (C_GR, T.TMR, 0, F32), (C_GR + 512, T.TMR, 512, F32),
               (C_NV, T.TMNV, 0, BF16), (C_NV + 512, T.TMNV, 512, BF16)]
        for bi, (c0, dst, dc, dt) in enumerate(tmb):
            wt, wr = wtm.next()
            for g in range(2):
                P.dma(wt[:, 8 * g:8 * g + 8, :], WinV[:, 8 * g:8 * g + 8, c0:c0 + 512], [], [(wr, g)])
            for i in range(17):
                M = 128 if i < 16 else 64
                pt, pr = psr.next()
                for kc in range(KC):
                    P.mm(pt[0:M, :], XT[:, kc, 128 * i:128 * i + M], wt[:, kc, :], kc == 0, kc == KC - 1, [(wr, kc // 8), ("XT", kc // 4)], [pr])
                sg, sr = (sb16 if dt == BF16 else sf32).next()
                if ev % 2 == 0:
                    P.copy("scalar", sg[0:M, :], pt[0:M, :], [pr], [sr])
                else:
                    P.copy("vector", sg[0:M, :], pt[0:M, :], [pr], [sr])
                ev += 1
                P.dma(dst[128 * i:128 * i + M, dc:dc + 512], sg[0:M, :], [sr], [("TM", bi, i)])
        P.emit()


def phaseG(nc, P, T, l):
    with contextlib.ExitStack() as st:
        sb, ps, ring, pring = _alloc(nc, st, f"G{l}_")
        QKG = sb("QKG", [128, 8, LP], BF16)
        for g in range(2):
            P.dma(QKG[:, 4 * g:4 * g + 4, :], T.QK_d[4 * g:4 * g + 4].rearrange("b k t -> k b t"), [], [("QKG", g)])
        if l < T.ffl:
            conv_weight(P, T.w_ff2_b[l], T.w_ff2[l], DFF, "cw_ff2", step=1024)
        if l + 1 < T.wl:
            conv_weight(P, T.w_in_b[l + 1], T.w_in[l + 1], D, "cw_in")
        LRT = []
        WUP = []
        for z in range(2):
            t = sb(f"LRT{z}", [17, LP], F32)
            P.memset("gpsimd", t[:, :], 1.0, [], [("LRT", z)])
            P.dma(t[0:16, :], T.LRT_d[z], [], [("LRT", z)])
            LRT.append(t)
            w = sb(f"WUP{z}", [17, 512], F32)
            P.dma(w[:, :], T.wup[l, z], [], [("WUP", z)])
            WUP.append(w)
        CST = sb("CST", [64, 6, 64], F32)
        P.dma(CST[:, :, :], T.cst, [], ["CST"])
        IDF = sb("IDF", [128, 128], F32)
        P.dma(IDF[:, :], T.ident, [], ["IDF"])
        NW4 = sb("NW4", [64, 1024], F32)
        P.dma(NW4[:, :], T.gnw[l], [], ["NW4"])
        YG = sb("YG", [128, 8, LP], BF16)
        S = sb("S", [128, 4, 256], F32)
        Sb = ring("Sb", 2, [128, 4, 256], BF16)
        kk = ring("kk", 2, [64, 512], BF16)
        vv = ring("vv", 2, [64, 1024], BF16)
        rr_ = ring("rr", 2, [64, 1024], F32)
        of_ = ring("of", 2, [64, 1024], F32)
        sp_ = ring("sp", 2, [64, 512], F32)
        EB_ = ring("EB", 2, [128, 256], F32)
        ENB_ = ring("ENB", 2, [128, 256], F32)
        ER_ = ring("ER", 2, [64, 512], F32)
        qt_ = ring("qt", 2, [128, 4, 64], BF16)
        kt_ = ring("kt", 2, [128, 4, 64], BF16)
        kh_ = ring("kh", 2, [64, 512], BF16)
        A_ = ring("A", 2, [64, 4, 64], BF16)
        gate_ = ring("gate", 2, [64, 1024], F32)
        y_ = ring("y", 2, [64, 1024], F32)
        sq_ = ring("sq", 2, [64, 256], F32)
        st4_ = ring("st4", 2, [64, 12], F32)
        psA = ps("gA", [128, 512])
        psB = ps("gB", [128, 256])
        psC = ps("gC", [64, 512])
        psD = ps("gD", [64, 256])
        psO = [ps("gO0", [64, 512]), ps("gO1", [64, 512])]
        psS = [ps("gS0", [128, 512]), ps("gS1", [128, 512])]

        for z in range(2):
            P.memset("vector", S[:, :, :], 0.0, [], ["S"])
            sbt, sbr = Sb.next()
            P.memset("vector", sbt[:, :, :], 0.0, [], [sbr])
            order = range(NCH) if z == 0 else range(NCH - 1, -1, -1)
            for c in order:
                t0 = 64 * c
                kkt, kkr = kk.next()
                P.dma(kkt[:, :], T.TMK[t0:t0 + 64, :], [], [kkr])
                vvt, vvr = vv.next()
                P.dma(vvt[:, :], T.TMV[t0:t0 + 64, :], [], [vvr])
                if z == 1:
                    rrt, rrr = rr_.next()
                    P.dma(rrt[:, :], T.TMR[t0:t0 + 64, :], [], [rrr])
                oft, ofr = of_.next()
                if z == 1:
                    P.dma(oft[:, :], T.OF_d[t0:t0 + 64, :], [], [ofr])
                P.mm(psA[0:64, :], LRT[z][0:17, t0:t0 + 64], WUP[z][0:17, :], True, True, [("LRT", z), ("WUP", z)], ["gA"])
                spt, spr = sp_.next()
                P.act(spt[:, :], psA[0:64, :], AF.Exp, ["gA"], [spr], scale=-1.0)
                P.act(spt[:, :], spt[:, :], AF.Ln, [spr], [spr], bias=1.0)
                for h in range(4):
                    P.mm(psB[:, 64 * h:64 * h + 64], spt[:, 128 * h:128 * h + 128], CST[:, z, :], True, True, [spr, "CST"], ["gB"])
                P.mm(psC[:, :], CST[:, 2 + z, :], spt[:, :], True, True, [spr, "CST"], ["gC"])
                EBt, EBr = EB_.next()
                ENBt, ENBr = ENB_.next()
                ERt, ERr = ER_.next()
                P.act(EBt[:, :], psB[:, :], AF.Exp, ["gB"], [EBr])
                P.act(ENBt[:, :], psB[:, :], AF.Exp, ["gB"], [ENBr], scale=-1.0)
                P.act(ERt[:, :], psC[:, :], AF.Exp, ["gC"], [ERr])
                qtt, qtr = qt_.next()
                ktt, ktr = kt_.next()
                kht, khr = kh_.next()
                P.tt("vector", qtt[:, :, :], QKG[:, 0:4, t0:t0 + 64], EBt[:, :].rearrange("p (h t) -> p h t", h=4), ALU.mult, [("QKG", 0), EBr], [qtr])
                P.tt("vector", ktt[:, :, :], QKG[:, 4:8, t0:t0 + 64], ENBt[:, :].rearrange("p (h t) -> p h t", h=4), ALU.mult, [("QKG", 1), ENBr], [ktr])
                P.tt("gpsimd", kht[:, :], kkt[:, :], ERt[:, :], ALU.mult, [kkr, ERr], [khr])
                for h in range(4):
                    P.mm(psD[:, 64 * h:64 * h + 64], ktt[:, h, :], qtt[:, h, :], True, True, [ktr, qtr], ["gD"])
                At, Ar = A_.next()
                P.tt("vector", At[:, :, :], psD[:, :].rearrange("p (h t) -> p h t", h=4),
                     CST[:, 4 + z, :].unsqueeze(1).to_broadcast([64, 4, 64]), ALU.mult, ["gD", "CST"], [Ar])
                sbt_old, sbr_old = sbt, sbr
                for h in range(4):
                    o_ap = psO[h // 2][:, 256 * (h % 2):256 * (h % 2) + 256]
                    P.mm(o_ap, At[:, h, :], vvt[:, 256 * h:256 * h + 256], True, False, [Ar, vvr], [("gO", h // 2)])
                    P.mm(o_ap, qtt[:, h, :], sbt_old[:, h, :], False, True, [qtr, sbr_old], [("gO", h // 2)])
                for h in range(4):
                    P.mm(psS[h // 2][:, 256 * (h % 2):256 * (h % 2) + 256], kht[:, 128 * h:128 * h + 128], vvt[:, 256 * h:256 * h + 256],
                         True, True, [khr, vvr], [("gS", h // 2)])
                for h in range(4):
                    col = 64 * h + (63 if z == 0 else 0)
                    P.stt("vector", S[:, h, :], S[:, h, :], EBt[:, col:col + 1], psS[h // 2][:, 256 * (h % 2):256 * (h % 2) + 256],
                          ALU.mult, ALU.add, ["S", EBr, ("gS", h // 2)], ["S"])
                sbt, sbr = Sb.next()
                P.copy("scalar", sbt[:, :, :], S[:, :, :], ["S"], [sbr])
                if z == 0:
                    for j in range(2):
                        P.copy("scalar", oft[:, 512 * j:512 * j + 512], psO[j][:, :], [("gO", j)], [ofr])
                    P.dma(T.OF_d[t0:t0 + 64, :], oft[:, :], [ofr], [("OF_d", c)])
                    continue
                for j in range(2):
                    P.tt("vector", oft[:, 512 * j:512 * j + 512], oft[:, 512 * j:512 * j + 512], psO[j][:, :], ALU.add, [ofr, ("gO", j)], [ofr])
                sqt, sqr = sq_.next()
                s4, s4r = st4_.next()
                P.memset("gpsimd", s4[:, 0:4], 0.0, [], [(s4r, "acc")])
                for h in range(4):
                    P.act(sqt[:, :], oft[:, 256 * h:256 * h + 256], AF.Square, [ofr, (s4r, "acc")], [sqr, (s4r, "acc", h)], accum_out=s4[:, h:h + 1])
                P.act(s4[:, 4:8], s4[:, 0:4], AF.Ln, [sqr, (s4r, "acc")] + [(s4r, "acc", h) for h in range(4)], [s4r], bias=EPS, scale=1.0 / 256.0)
                P.act(s4[:, 8:12], s4[:, 4:8], AF.Exp, [s4r], [s4r], scale=-0.5)
                gt, gr = gate_.next()
                P.act(gt[:, :], rrt[:, :], AF.Exp, [rrr], [gr], scale=-1.0)
                P.ts("gpsimd", gt[:, :], gt[:, :], 1.0, ALU.add, [gr], [gr])
                P.add("vector", (lambda e, a=gt: e.reciprocal(a[:, :], a[:, :])), [gr], [gr])
                P.tt("gpsimd", gt[:, :], gt[:, :], rrt[:, :], ALU.mult, [gr, rrr], [gr])
                P.tt("gpsimd", gt[:, :], gt[:, :], NW4[:, :], ALU.mult, [gr, "NW4"], [gr])
                yt, yr = y_.next()
                for h in range(4):
                    P.stt("vector", yt[:, 256 * h:256 * h + 256], oft[:, 256 * h:256 * h + 256], s4[:, 8 + h:9 + h],
                          gt[:, 256 * h:256 * h + 256], ALU.mult, ALU.mult, [ofr, s4r, gr], [yr])
                for j in range(8):
                    P.tr(psA[:, 64 * j:64 * j + 64], yt[:, 128 * j:128 * j + 128], IDF[0:64, 0:64], [yr, "IDF"], ["gA"])
                P.copy("scalar", YG[:, :, t0:t0 + 64], psA[:, :].rearrange("p (j t) -> p j t", j=8), ["gA"], [("YG", c)])
        P.dma(T.XT_d[:, 0:8, :], YG[:, :, :], [("YG", c) for c in range(NCH)], ["XTd_g"])
        P.emit()


def phaseN(nc, P, T, l):
    with contextlib.ExitStack() as st:
        sb, ps, ring, pring = _alloc(nc, st, f"N{l}_")
        Wtab = sb("Wtab", [64, 16, 15, 64], F32)
        NM = sb("NM", [64, 64], F32)
        IDF = sb("IDF", [128, 128], F32)
        P.dma(NM[:, :], T.nmask, [], ["NM"])
        P.dma(IDF[:, :], T.ident, [], ["IDF"])
        for j in range(4):
            wv = Wtab[:, 4 * j:4 * j + 4, :, :]
            P.dma(wv, T.bt[l][:, 4 * j:4 * j + 4, :, :], [], [("Wtab", j)])
            P.act(wv, wv, AF.Exp, [("Wtab", j)], [("Wtab", j)])
            wv3 = wv.rearrange("p h m c -> p (h m) c")
            P.tt("vector", wv3, wv3, NM[:, :].unsqueeze(1).to_broadcast([64, 60, 64]), ALU.mult, [("Wtab", j), "NM"], [("Wtab", j)])
        qT_ = ring("nqT", 2, [128, LP], BF16)
        kT_ = ring("nkT", 2, [128, LP], BF16)
        vA_ = ring("nvA", 2, [64, NCH, 2, 65], BF16)
        vM_ = ring("nvM", 2, [16, 2, 65], BF16)
        YP_ = ring("nYP", 2, [64, NCH, 128], F32)
        YM_ = ring("nYM", 2, [16, 128], F32)
        YT_ = ring("nYT", 2, [128, LP], BF16)
        E32_ = ring("nE32", 3, [64, 512], F32)
        Eb_ = ring("nEb", 3, [64, 8, 64], BF16)
        EM_ = ring("nEM", 3, [16, 64], BF16)
        rc_ = ring("nrc", 4, [64, 1], F32)
        for t in vA_.tiles:
            P.memset("gpsimd", t[:, :, :, :], 1.0, [], [("nvA", vA_.tiles.index(t))])
        for t in vM_.tiles:
            P.memset("gpsimd", t[:, :, :], 1.0, [], [("nvM", vM_.tiles.index(t))])
        pS_ = pring("nS", 2, [64, 512])
        pM_ = pring("nM", 2, [64, 512])
        pO_ = pring("nO", 2, [64, 512])
        pT_ = pring("nT", 2, [128, 512])
        for hp in range(8):
            qT, qr = qT_.next()
            kT, kr_ = kT_.next()
            P.dma(qT[:, :], T.QK_d[8 + hp], [], [qr])
            P.dma(kT[:, :], T.QK_d[16 + hp], [], [kr_])
            vA, vr = vA_.next()
            for hh in range(2):
                P.dma(vA[:, :, hh, 0:64], T.TMNV[:, 128 * hp + 64 * hh:128 * hp + 64 * hh + 64].rearrange("(c t) d -> t c d", t=64), [], [vr])
            vM, vmr = vM_.next()
            for hh in range(2):
                P.dma(vM[:, hh, 0:64], T.TMNV[48:64, 128 * hp + 64 * hh:128 * hp + 64 * hh + 64], [], [vmr])
            YP, ypr = YP_.next()
            YM, ymr = YM_.next()
            YT, ytr = YT_.next()
            P.memset("gpsimd", YT[:, 0:48], 0.0, [], [(ytr, "pad")])
            for hh in range(2):
                po = 64 * hh
                pm, pmr = pM_.next()
                P.mm(pm[0:16, 0:16], kT[po:po + 64, 48:64], qT[po:po + 64, 48:64], True, True, [kr_, qr], [pmr])
                em, emr = EM_.next()
                P.act(em[0:16, 0:16], pm[0:16, 0:16], AF.Exp, [pmr], [emr])
                po_, por = pO_.next()
                P.mm(po_[0:16, 0:65], em[0:16, 0:16], vM[0:16, hh, :], True, True, [emr, vmr], [por])
                rc, rcr = rc_.next()
                P.add("vector", (lambda e, a=rc, b=po_: e.reciprocal(a[0:16, :], b[0:16, 64:65])), [por], [rcr])
                P.ts("vector", YM[0:16, 64 * hh:64 * hh + 64], po_[0:16, 0:64], rc[0:16, 0:1], ALU.mult, [por, rcr], [(ymr, hh)])
            for r in range(32):
                c = r + 1
                rs = min(max(r - 4, 0), 24)
                dl = r - rs
                for hh in range(2):
                    h = 2 * hp + hh
                    po = 64 * hh
                    qsl = qT[po:po + 64, 64 * c:64 * c + 64]
                    pS, psr = pS_.next()
                    for kr in range(8):
                        kc_ = rs + kr + 1
                        P.mm(pS[:, 64 * kr:64 * kr + 64], kT[po:po + 64, 64 * kc_:64 * kc_ + 64], qsl, True, True, [kr_, qr], [psr])
                    pm, pmr = pM_.next()
                    P.mm(pm[0:16, 0:64], kT[po:po + 64, 48:64], qsl, True, True, [kr_, qr], [pmr])
                    e32, e32r = E32_.next()
                    P.act(e32[:, :], pS[:, :], AF.Exp, [psr], [e32r])
                    em, emr = EM_.next()
                    P.act(em[:, :], pm[0:16, 0:64], AF.Exp, [pmr], [emr])
                    eb, ebr = Eb_.next()
                    P.tt("vector", eb[:, :, :], e32[:, :].rearrange("p (k c) -> p k c", k=8), Wtab[:, h, 7 - dl:15 - dl, :], ALU.mult,
                         [e32r, ("Wtab", h // 4)], [ebr])
                    po_, por = pO_.next()
                    for kr in range(8):
                        P.mm(po_[:, 0:65], eb[:, kr, :], vA[:, rs + kr + 1, hh, :], kr == 0, False, [ebr, vr], [por])
                    P.mm(po_[:, 0:65], em[:, :], vM[:, hh, :], False, True, [emr, vmr], [por])
                    rc, rcr = rc_.next()
                    P.add("vector", (lambda e, a=rc, b=po_: e.reciprocal(a[:, :], b[:, 64:65])), [por], [rcr])
                    P.ts("vector", YP[:, c, 64 * hh:64 * hh + 64], po_[:, 0:64], rc[:, 0:1], ALU.mult, [por, rcr], [(ypr, c, hh)])
            pt, ptr = pT_.next()
            P.tr(pt[:, 0:16], YM[0:16, :], IDF[0:16, 0:16], [(ymr, 0), (ymr, 1), "IDF"], [ptr])
            P.copy("scalar", YT[:, 48:64], pt[:, 0:16], [ptr], [(ytr, "m")])
            for g in range(4):
                pt, ptr = pT_.next()
                for j in range(8):
                    c = 1 + 8 * g + j
                    P.tr(pt[:, 64 * j:64 * j + 64], YP[:, c, :], IDF[0:64, 0:64], [(ypr, c, 0), (ypr, c, 1), "IDF"], [ptr])
                P.copy("scalar", YT[:, 64 + 512 * g:64 + 512 * g + 512], pt[:, :], [ptr], [(ytr, g)])
            P.dma(T.XT_d[:, 8 + hp, :], YT[:, :], [(ytr, "pad"), (ytr, "m")] + [(ytr, g) for g in range(4)], [("XTd_n", hp)])
        P.emit()


def layer_norm_tail(P, z, zr, M, stats, mv, sm, LNW, LNB, smr):
    P.add("vector", (lambda e: e.bn_aggr(mv[0:M, :], stats[0:M, :, :])), [zr], [smr])
    P.act(sm[0:M, 0:1], mv[0:M, 1:2], AF.Ln, [smr], [smr], bias=EPS, scale=1.0)
    P.act(sm[0:M, 1:2], sm[0:M, 0:1], AF.Exp, [smr], [smr], scale=-0.5)
    P.stt("vector", sm[0:M, 2:3], mv[0:M, 0:1], -1.0, sm[0:M, 1:2], ALU.mult, ALU.mult, [smr], [smr])
    P.act(z[0:M, :], z[0:M, :], AF.Identity, [zr, smr], [zr], bias=sm[0:M, 2:3], scale=sm[0:M, 1:2])
    P.tt("gpsimd", z[0:M, :], z[0:M, :], LNW[0:M, :], ALU.mult, [zr, "LNW"], [zr])
    P.tt("gpsimd", z[0:M, :], z[0:M, :], LNB[0:M, :], ALU.add, [zr, "LNB"], [zr])


def phaseC(nc, P, T, l):
    with contextlib.ExitStack() as st:
        sb, ps, ring, pring = _alloc(nc, st, f"C{l}_")
        WO = sb("WO", [128, KC, D], BF16)
        WoV = T.w_out_b[l].rearrange("(kc k) c -> k kc c", k=128)
        for g in range(4):
            P.dma(WO[:, 4 * g:4 * g + 4, :], WoV[:, 4 * g:4 * g + 4, :], [], [("WO", g)])
        LNW = sb("LNW", [128, D], F32)
        LNB = sb("LNB", [128, D], F32)
        IDB = sb("IDB", [128, 128], BF16)
        P.dma(LNW[:, :], T.ln1w[l], [], ["LNW"])
        P.dma(LNB[:, :], T.ln1b[l], [], ["LNB"])
        P.dma(IDB[:, :], T.ident, [], ["IDB"], q="gpsimd")
        mt_ = ring("cmt", 2, [128, KC, 128], BF16)
        hr_ = ring("chr", 2, [128, D], F32)
        z_ = ring("cz", 2, [128, D], F32)
        zb_ = ring("czb", 2, [128, D], BF16)
        stg_ = ring("cstg", 2, [128, KC, 128], BF16)
        stats_ = ring("cstat", 2, [128, 4, 6], F32)
        mv_ = ring("cmv", 2, [128, 2], F32)
        sm_ = ring("csm", 2, [128, 4], F32)
        pz_ = pring("cpz", 6, [128, 512])
        pT_ = pring("cpT", 1, [128, KC, 128], BF16)
        for i in range(17):
            M = 128 if i < 16 else 64
            mt, mr = mt_.next()
            P.dma(mt[:, :, 0:M], T.XT_d[:, :, 128 * i:128 * i + M], [("XTd", i)], [mr])
            hr, hrr = hr_.next()
            P.dma(hr[0:M, :], T.Hres[128 * i:128 * i + M, :], [("Hres", i)], [hrr])
            z, zr = z_.next()
            stats, sr_ = stats_.next()
            for cb in range(4):
                pz, pzr = pz_.next()
                for kc in range(KC):
                    P.mm(pz[0:M, :], mt[:, kc, 0:M], WO[:, kc, 512 * cb:512 * cb + 512], kc == 0, kc == KC - 1, [mr, ("WO", kc // 4)], [pzr])
                zs = z[0:M, 512 * cb:512 * cb + 512]
                P.stt("vector", zs, hr[0:M, 512 * cb:512 * cb + 512], ALPHA, pz[0:M, :], ALU.mult, ALU.add, [hrr, pzr], [zr])
                P.add("vector", (lambda e, a=stats, b=zs, cb=cb, M=M: e.bn_stats(a[0:M, cb, :], b)), [zr], [zr])
            mv, mvr = mv_.next()
            sm, smr = sm_.next()
            layer_norm_tail(P, z, zr, M, stats, mv, sm, LNW, LNB, smr)
            P.dma(T.Hres[128 * i:128 * i + M, :], z[0:M, :], [zr], [("Hres", i)])
            zb, zbr = zb_.next()
            P.copy("scalar", zb[0:M, :], z[0:M, :], [zr], [zbr])
            pT, pTr = pT_.next()
            for kc in range(KC):
                P.tr(pT[:, kc, 0:M], zb[0:M, 128 * kc:128 * kc + 128], IDB[0:M, 0:M], [zbr, "IDB"], [pTr])
            sg, sgr = stg_.next()
            P.copy("vector", sg[:, :, 0:M], pT[:, :, 0:M], [pTr], [sgr])
            P.dma(T.XT_d[:, :, 128 * i:128 * i + M], sg[:, :, 0:M], [sgr], [("XTd", i)])
        P.emit()


def phaseD(nc, P, T, l, last):
    with contextlib.ExitStack() as st:
        sb, ps, ring, pring = _alloc(nc, st, f"D{l}_")
        LNW = sb("LNW", [128, D], F32)
        LNB = sb("LNB", [128, D], F32)
        IDB = sb("IDB", [128, 128], BF16)
        IDF = sb("IDF", [128, 128], F32)
        P.dma(LNW[:, :], T.ln2w[l], [], ["LNW"])
        P.dma(LNB[:, :], T.ln2b[l], [], ["LNB"])
        P.dma(IDB[:, :], T.ident, [], ["IDB"], q="gpsimd")
        P.dma(IDF[:, :], T.ident, [], ["IDF"])
        NT = 448
        xt_ = ring("dxt", 1, [128, KC, NT], BF16)
        uT = sb("duT", [128, 64, NT], BF16)
        w1_ = ring("dw1", 2, [128, KC, 256], BF16)
        w2_ = ring("dw2", 3, [128, 32, 128], BF16)
        Y2T = sb("dY2T", [128, KC, NT], F32)
        rl_ = ring("drl", 2, [128, NT], F32)
        hr_ = ring("dhr", 1, [128, D], F32)
        z_ = ring("dz", 2, [128, D], F32)
        zb_ = ring("dzb", 1, [128, D], BF16)
        stg_ = ring("dstg", 1, [128, KC, 128], BF16)
        stats_ = ring("dstat", 2, [128, 4, 6], F32)
        mv_ = ring("dmv", 2, [128, 2], F32)
        sm_ = ring("dsm", 2, [128, 4], F32)
        pm_ = pring("dpm", 3, [128, 512])
        pz_ = pring("dpz", 3, [128, 512])
        pT_ = pring("dpT", 1, [128, KC, 128], BF16)
        W1v = T.w_ff1_b[l].rearrange("(kc k) f -> k kc f", k=128)
        W2v = T.w_ff2_b[l].rearrange("(fc f) d -> f fc d", f=128)
        TGS = [(0, 448), (448, 448), (896, 448), (1344, 384), (1728, 384)]
        ev = 0
        for (t0, n) in TGS:
            xt, xr = xt_.next()
            P.dma(xt[:, :, 0:n], T.XT_d[:, :, t0:t0 + n], [("XTd", t0 + 128 * s) for s in range((n + 127) // 128)], [xr])
            for fb2 in range(32):
                w1, w1r = w1_.next()
                P.dma(w1[:, :, :], W1v[:, :, 256 * fb2:256 * fb2 + 256], [], [w1r])
                for j in range(2):
                    fb = 2 * fb2 + j
                    pm, pmr = pm_.next()
                    for kc in range(KC):
                        P.mm(pm[:, 0:n], w1[:, kc, 128 * j:128 * j + 128], xt[:, kc, 0:n], kc == 0, kc == KC - 1, [w1r, xr], [pmr])
                    rl, rlr = rl_.next()
                    P.act(rl[:, 0:n], pm[:, 0:n], AF.Relu, [pmr], [rlr])
                    P.tt("vector" if ev % 2 == 0 else "gpsimd", uT[:, fb, 0:n], rl[:, 0:n], rl[:, 0:n], ALU.mult, [rlr], [("uT", fb)])
                    ev += 1
            for db in range(16):
                pm, pmr = pm_.next()
                for half in range(2):
                    w2, w2r = w2_.next()
                    P.dma(w2[:, :, :], W2v[:, 32 * half:32 * half + 32, 128 * db:128 * db + 128], [], [w2r])
                    for f in range(32):
                        fc = 32 * half + f
                        P.mm(pm[:, 0:n], w2[:, f, :], uT[:, fc, 0:n], fc == 0, fc == 63, [w2r, ("uT", fc)], [pmr])
                if db % 2 == 0:
                    P.copy("scalar", Y2T[:, db, 0:n], pm[:, 0:n], [pmr], [("Y2T", db)])
                else:
                    P.copy("vector", Y2T[:, db, 0:n], pm[:, 0:n], [pmr], [("Y2T", db)])
            for s in range((n + 127) // 128):
                m = min(128, n - 128 * s)
                p0 = t0 + 128 * s
                hr, hrr = hr_.next()
                P.dma(hr[0:m, :], T.Hres[p0:p0 + m, :], [("Hres", p0)], [hrr])
                z, zr = z_.next()
                stats, sr_ = stats_.next()
                for cb in range(4):
                    pz, pzr = pz_.next()
                    for dj in range(4):
                        db = 4 * cb + dj
                        P.tr(pz[0:m, 128 * dj:128 * dj + 128], Y2T[:, db, 128 * s:128 * s + m], IDF[:, :], [("Y2T", db), "IDF"], [pzr])
                    zs = z[0:m, 512 * cb:512 * cb + 512]
                    P.stt("vector", zs, hr[0:m, 512 * cb:512 * cb + 512], ALPHA, pz[0:m, :], ALU.mult, ALU.add, [hrr, pzr], [zr])
                    P.add("vector", (lambda e, a=stats, b=zs, cb=cb, m=m: e.bn_stats(a[0:m, cb, :], b)), [zr], [zr])
                mv, mvr = mv_.next()
                sm, smr = sm_.next()
                layer_norm_tail(P, z, zr, m, stats, mv, sm, LNW, LNB, smr)
                if last:
                    r0 = max(p0, 64)
                    if r0 < p0 + m:
                        P.dma(T.out[r0 - 64:p0 + m - 64, :], z[r0 - p0:m, :], [zr], [("out", p0)])
                    continue
                P.dma(T.Hres[p0:p0 + m, :], z[0:m, :], [zr], [("Hres", p0)])
                zb, zbr = zb_.next()
                P.copy("scalar", zb[0:m, :], z[0:m, :], [zr], [zbr])
                pT, pTr = pT_.next()
                for kc in range(KC):
                    P.tr(pT[:, kc, 0:m], zb[0:m, 128 * kc:128 * kc + 128], IDB[0:m, 0:m], [zbr, "IDB"], [pTr])
                sg, sgr = stg_.next()
                P.copy("vector", sg[:, :, 0:m], pT[:, :, 0:m], [pTr], [sgr])
                if p0 == 0:
                    P.memset("vector", sg[:, :, 0:48], 0.0, [sgr], [sgr])
                P.dma(T.XT_d[:, :, p0:p0 + m], sg[:, :, 0:m], [sgr], [("XTd", p0)])
        P.emit()


def build_program(upto=None, dbg=(), wl=DEPTH, ffl=DEPTH):
    nc = bass.Bass("TRN2", target_bir_lowering=False)
    T = Ctx()

    def din(name, shape, dt=F32):
        return nc.dram_tensor(name, list(shape), dt, kind="ExternalInput").ap()

    def dscr(name, shape, dt):
        kind = "ExternalOutput" if name in dbg else "Internal"
        return nc.dram_tensor(name, list(shape), dt, kind=kind).ap()

    T.x = din("x", [2048, D])
    T.meta = din("meta", [16, D])
    T.wl, T.ffl = wl, ffl
    T.w_in = din("w_in", [wl, D, PT])
    T.w_out = din("w_out", [wl, D, D])
    T.w_ff1 = din("w_ff1", [ffl, D, DFF]) if ffl else None
    T.w_ff2 = din("w_ff2", [ffl, DFF, D]) if ffl else None
    T.wup = din("wup", [DEPTH, 2, 17, 512])
    T.gnw = din("gnw", [DEPTH, 64, 1024])
    T.bt = din("bt", [DEPTH, 64, 16, 15, 64])
    T.nmask = din("nmask", [64, 64])
    T.cst = din("cst", [64, 6, 64])
    T.ident = din("ident", [128, 128])
    T.ln1w = din("ln1w", [DEPTH, 128, D])
    T.ln1b = din("ln1b", [DEPTH, 128, D])
    T.ln2w = din("ln2w", [DEPTH, 128, D])
    T.ln2b = din("ln2b", [DEPTH, 128, D])
    T.out = nc.dram_tensor("out", [2048, D], F32, kind="ExternalOutput").ap()
    T.Hres = dscr("Hres", [LP, D], F32)
    T.XT_d = dscr("XT_d", [128, KC, LP], BF16)
    T.w_in_b = [dscr(f"w_in_b{l}", [D, PT], BF16) for l in range(DEPTH)]
    T.w_out_b = [dscr(f"w_out_b{l}", [D, D], BF16) for l in range(DEPTH)]
    T.w_ff1_b = [dscr(f"w_ff1_b{l}", [D, DFF], BF16) for l in range(DEPTH)]
    T.w_ff2_b = [dscr(f"w_ff2_b{l}", [DFF, D], BF16) for l in range(DEPTH)]
    T.QK_d = dscr("QK_d", [24, 128, LP], BF16)
    T.LRT_d = dscr("LRT_d", [2, 16, LP], F32)
    T.TMK = dscr("TMK", [LP, 512], BF16)
    T.TMV = dscr("TMV", [LP, 1024], BF16)
    T.TMR = dscr("TMR", [LP, 1024], F32)
    T.TMNV = dscr("TMNV", [LP, 1024], BF16)
    T.OF_d = dscr("OF_d", [LP, 1024], F32)

    phases = [("0", lambda P: phase0(nc, P, T))]
    for l in range(DEPTH):
        phases.append((f"A{l}", lambda P, l=l: phaseA(nc, P, T, l)))
        phases.append((f"G{l}", lambda P, l=l: phaseG(nc, P, T, l)))
        phases.append((f"N{l}", lambda P, l=l: phaseN(nc, P, T, l)))
        phases.append((f"C{l}", lambda P, l=l: phaseC(nc, P, T, l)))
        phases.append((f"D{l}", lambda P, l=l: phaseD(nc, P, T, l, l == DEPTH - 1)))
    with contextlib.ExitStack() as st:
        P = Prog(nc, st)
        for name, fn in phases:
            fn(P)
            if upto is not None and name == upto:
                break
    return nc


def host_inputs(inputs):
    f32 = np.float32
    s = np.arange(64)
    le = (s[:, None] <= s[None, :]).astype(f32)
    ge = (s[:, None] >= s[None, :]).astype(f32)
    gt = (s[:, None] > s[None, :]).astype(f32)
    lt = (s[:, None] < s[None, :]).astype(f32)
    g = f32(-1.0 / 16.0)
    cst = np.stack([le * g, ge * g, gt * g, lt * g, le, ge], axis=1).astype(f32)
    kc = np.arange(64)[:, None]
    c = np.arange(64)[None, :]
    cs = np.clip(c - 8, 0, 48)
    nmask = ((kc >= cs) & (kc < cs + 16)).astype(f32)
    dc = np.clip(kc - c, -15, 15) + 15
    rb = np.asarray(inputs["na_rel_bias"], f32)
    bt = rb[:, :, :, dc]
    bt = np.ascontiguousarray(bt.transpose(0, 3, 1, 2, 4))
    wup = np.concatenate([np.asarray(inputs["gla_w_up"], f32), np.asarray(inputs["gla_b_up"], f32)[:, :, None, :]], axis=2)
    gnw = np.ascontiguousarray(np.broadcast_to(np.tile(np.asarray(inputs["gla_norm_w"], f32), (1, 4))[:, None, :], (DEPTH, 64, 1024)))
    rep = lambda a: np.ascontiguousarray(np.broadcast_to(np.asarray(a, f32)[:, None, :], (DEPTH, 128, D)))
    shared = {
        "meta": np.asarray(inputs["meta"], f32),
        "w_in": np.asarray(inputs["w_in"], f32),
        "w_out": np.asarray(inputs["w_out"], f32),
        "w_ff1": np.asarray(inputs["w_ff1"], f32),
        "w_ff2": np.asarray(inputs["w_ff2"], f32),
        "wup": np.ascontiguousarray(wup),
        "gnw": gnw,
        "bt": bt,
        "nmask": nmask,
        "cst": cst,
        "ident": np.eye(128, dtype=f32),
        "ln1w": rep(inputs["ln1_w"]),
        "ln1b": rep(inputs["ln1_b"]),
        "ln2w": rep(inputs["ln2_w"]),
        "ln2b": rep(inputs["ln2_b"]),
    }
    return shared


def kernel(**inputs):
    x = np.asarray(inputs["x"], np.float32)
    shared = host_inputs(inputs)
    nc = build_program()
    in_maps = []
    for b in range(NCORES):
        m = dict(shared)
        m["x"] = np.ascontiguousarray(x[b])
        in_maps.append(m)
    res = run_bass_kernel_spmd(nc, in_maps, core_ids=list(range(NCORES)))
    return np.stack([np.asarray(r["out"], np.float32) for r in res.results], axis=0)
```

```python
import contextlib
import numpy as np
import concourse.bass as bass
import concourse.mybir as mybir
from concourse.bass_utils import run_bass_kernel_spmd

F32 = mybir.dt.float32
BF16 = mybir.dt.bfloat16
AF = mybir.ActivationFunctionType
ALU = mybir.AluOpType

D = 2048
LP = 2112
NCH = 33
KC = 16
DFF = 8192
PT = 6176
C_GQ, C_GK, C_GV, C_GR, C_LR, C_NQ, C_NK, C_NV = 0, 512, 1024, 2048, 3072, 3104, 4128, 5152
ALPHA = 4.0 ** 0.25
EPS = 1e-5
DEPTH = 2
NCORES = 4

ENGS = ("tensor", "vector", "scalar", "gpsimd", "sync")


class Prog:
    def __init__(self, nc, stack, n_dma=20, sync_same=True):
        self.nc = nc
        self.n_dma = n_dma
        self.sync_same = sync_same
        self.esem = {e: stack.enter_context(nc.semaphore("s_" + e)) for e in ENGS}
        self.dq = ("sync", "gpsimd")
        self.dsem = {q: [stack.enter_context(nc.semaphore(f"d_{q}_{k}")) for k in range(n_dma)] for q in self.dq}
        self.cnt = {e: 0 for e in ENGS}
        self.dma_k = {q: 0 for q in self.dq}
        self.dval = {q: [0] * n_dma for q in self.dq}
        self.ops = []
        self.nops = 0

    def add(self, eng, fn, reads=(), writes=(), dma=False):
        self.ops.append((eng, fn, tuple(reads), tuple(writes), dma))

    def dma(self, out, in_, r, w, q="sync"):
        self.add(q, lambda e: e.dma_start(out=out, in_=in_), r, w, dma=True)

    def mm(self, out, lhsT, rhs, start, stop, r, w):
        self.add("tensor", lambda e: e.matmul(out, lhsT, rhs, start=start, stop=stop), r, w)

    def tr(self, out, in_, ident, r, w):
        self.add("tensor", lambda e: e.transpose(out, in_, ident), r, w)

    def act(self, out, in_, func, r, w, bias=None, scale=None, accum_out=None):
        kw = {}
        if bias is not None:
            kw["bias"] = bias
        if scale is not None:
            kw["scale"] = scale
        if accum_out is not None:
            kw["accum_out"] = accum_out
        self.add("scalar", lambda e: e.activation(out, in_, func, **kw), r, w)

    def copy(self, eng, out, in_, r, w):
        if eng == "scalar":
            self.add(eng, lambda e: e.copy(out, in_), r, w)
        else:
            self.add(eng, lambda e: e.tensor_copy(out, in_), r, w)

    def tt(self, eng, out, in0, in1, op, r, w):
        self.add(eng, lambda e: e.tensor_tensor(out, in0, in1, op), r, w)

    def ts(self, eng, out, in0, s1, op0, r, w, s2=None, op1=None):
        if op1 is None:
            self.add(eng, lambda e: e.tensor_scalar(out, in0, s1, None, op0), r, w)
        else:
            self.add(eng, lambda e: e.tensor_scalar(out, in0, s1, s2, op0, op1), r, w)

    def stt(self, eng, out, in0, scalar, in1, op0, op1, r, w):
        self.add(eng, lambda e: e.scalar_tensor_tensor(out, in0, scalar, in1, op0, op1), r, w)

    def memset(self, eng, ap, val, r, w):
        self.add(eng, lambda e: e.memset(ap, val), r, w)

    def emit(self):
        nc = self.nc
        ops = self.ops
        self.ops = []
        n = len(ops)
        self.nops += n
        last_w = {}
        readers = {}
        deps = [None] * n
        for i, (eng, fn, reads, writes, dma) in enumerate(ops):
            d = set()
            for r in reads:
                j = last_w.get(r)
                if j is not None:
                    d.add(j)
            for w in writes:
                j = last_w.get(w)
                if j is not None:
                    d.add(j)
                rr = readers.get(w)
                if rr:
                    d.update(rr)
            d.discard(i)
            for r in reads:
                readers.setdefault(r, []).append(i)
            for w in writes:
                last_w[w] = i
                readers[w] = []
            deps[i] = d
        signal = [False] * n
        need = [None] * n
        for i in range(n):
            eng_i = ops[i][0]
            dma_i = ops[i][4]
            lst = []
            for j in deps[i]:
                eng_j = ops[j][0]
                dma_j = ops[j][4]
                if not dma_j and not dma_i and eng_j == eng_i:
                    if eng_i == "tensor" or not self.sync_same:
                        continue
                lst.append(j)
                signal[j] = True
            need[i] = lst
        last_on = {}
        for i in range(n):
            if not ops[i][4]:
                last_on[ops[i][0]] = i
        for e, i in last_on.items():
            signal[i] = True
        count = [0] * n
        dslot = [None] * n
        for i in range(n):
            eng, fn, reads, writes, dma = ops[i]
            if dma:
                k = self.dma_k[eng]
                self.dma_k[eng] += 1
                s = k % self.n_dma
                self.dval[eng][s] += 16
                dslot[i] = (eng, s, self.dval[eng][s])
            elif signal[i]:
                self.cnt[eng] += 1
                count[i] = self.cnt[eng]
        fin_cnt = dict(self.cnt)
        fin_dval = {q: list(v) for q, v in self.dval.items()}
        esem, dsem = self.esem, self.dsem

        def make(engname):
            def body(eng):
                waited = {}

                def do_wait(key, sem, val):
                    if waited.get(key, 0) >= val:
                        return
                    waited[key] = val
                    eng.wait_ge(sem, val)

                for i in range(n):
                    e_i, fn, reads, writes, dma = ops[i]
                    if e_i != engname:
                        continue
                    waits = {}
                    for j in need[i]:
                        if ops[j][4]:
                            qe, s, val = dslot[j]
                            key = ("d", qe, s)
                            sem = dsem[qe][s]
                        else:
                            key = ("e", ops[j][0])
                            sem = esem[ops[j][0]]
                            val = count[j]
                        if waits.get(key, (None, 0))[1] < val:
                            waits[key] = (sem, val)
                    if dma:
                        qe, s, val = dslot[i]
                        if val > 16:
                            key = ("d", qe, s)
                            if waits.get(key, (None, 0))[1] < val - 16:
                                waits[key] = (dsem[qe][s], val - 16)
                    for key, (sem, val) in waits.items():
                        do_wait(key, sem, val)
                    ins = fn(eng)
                    if dma:
                        qe, s, val = dslot[i]
                        ins.then_inc(dsem[qe][s], 16)
                    elif signal[i]:
                        ins.then_inc(esem[engname], 1)
                for e in ENGS:
                    if fin_cnt[e] > 0:
                        do_wait(("e", e), esem[e], fin_cnt[e])
                for q in self.dq:
                    for s in range(self.n_dma):
                        if fin_dval[q][s] > 0:
                            do_wait(("d", q, s), dsem[q][s], fin_dval[q][s])

            return body

        with nc.Block() as block:
            block.tensor(make("tensor"))
            block.vector(make("vector"))
            block.scalar(make("scalar"))
            block.gpsimd(make("gpsimd"))
            block.sync(make("sync"))


class Ring:
    def __init__(self, tiles, name):
        self.tiles = tiles
        self.name = name
        self.i = -1

    def next(self):
        self.i += 1
        k = self.i % len(self.tiles)
        return self.tiles[k], (self.name, k)


class Ctx:
    pass


def _alloc(nc, st, pfx=""):
    def sb(name, shape, dt):
        return st.enter_context(nc.sbuf_tensor(pfx + name, list(shape), dt))

    def ps(name, shape, dt=F32):
        return st.enter_context(nc.psum_tensor(pfx + name, list(shape), dt))

    def ring(name, n, shape, dt):
        return Ring([sb(f"{name}{k}", shape, dt) for k in range(n)], name)

    def pring(name, n, shape, dt=F32):
        return Ring([ps(f"{name}{k}", shape, dt) for k in range(n)], name)

    return sb, ps, ring, pring


def conv_weight(P, dst, src, rows, name, step=512):
    for r0 in range(0, rows, step):
        P.dma(dst[r0:r0 + step, :], src[r0:r0 + step, :], [], [(name, r0)], q="gpsimd")


def phase0(nc, P, T):
    import os
    MODE = os.environ.get("DBG0", "all")
    with contextlib.ExitStack() as st:
        sb, ps, ring, pring = _alloc(nc, st, "p0_")
        zero = sb("zero", [48, D], F32)
        idb = sb("idb", [128, 128], BF16)
        P.memset("gpsimd", zero[:, :], 0.0, [], ["zero"])
        P.dma(T.Hres[0:48, :], zero[:, :], ["zero"], ["HresP"], q="gpsimd")
        P.dma(T.Hres[48:64, :], T.meta, [], ["HresM"], q="gpsimd")
        for j in range(4):
            P.dma(T.Hres[64 + 512 * j:64 + 512 * (j + 1), :], T.x[512 * j:512 * (j + 1), :], [], [("HresX", j)], q="gpsimd")
        P.dma(idb[:, :], T.ident, [], ["idb"], q="gpsimd")
        if MODE in ("all", "conv"):
            conv_weight(P, T.w_in_b[0], T.w_in[0], D, "cw_in0")
        if MODE in ("conv", "init"):
            P.emit()
            return
        x32 = ring("x32", 2, [128, D], F32)
        xb = ring("xb", 2, [128, D], BF16)
        pT = pring("pT", 2, [128, KC, 128], BF16)
        stg = ring("stg", 2, [128, KC, 128], BF16)
        tiles = [int(v) for v in os.environ.get("DBGT", ",".join(str(v) for v in range(17))).split(",")]
        for i in tiles:
            M = 128 if i < 16 else 64
            xt, xr = x32.next()
            if i == 0:
                P.memset("vector", xt[0:64, :], 0.0, [], [xr])
                P.dma(xt[48:64, :], T.meta, [xr], [(xr, "m")])
                P.dma(xt[64:128, :], T.x[0:64, :], [], [(xr, "x")])
                rd = [xr, (xr, "m"), (xr, "x")]
            else:
                P.dma(xt[0:M, :], T.x[128 * i - 64:128 * i - 64 + M, :], [], [xr, (xr, "m"), (xr, "x")])
                rd = [xr]
            xbt, xbr = xb.next()
            P.copy("scalar", xbt[0:M, :], xt[0:M, :], rd, [xbr])
            pt, pr = pT.next()
            for kc in range(KC):
                P.tr(pt[:, kc, 0:M], xbt[0:M, 128 * kc:128 * (kc + 1)], idb[0:M, 0:M], [xbr, "idb"], [pr])
            sg, sr = stg.next()
            P.copy("vector", sg[:, :, 0:M], pt[:, :, 0:M], [pr], [sr])
            P.dma(T.XT_d[:, :, 128 * i:128 * i + M], sg[:, :, 0:M], [sr], [("XTd", i)])
        P.emit()


def phaseA(nc, P, T, l):
    with contextlib.ExitStack() as st:
        sb, ps, ring, pring = _alloc(nc, st, f"A{l}_")
        XT = sb("XT", [128, KC, LP], BF16)
        for g in range(4):
            P.dma(XT[:, 4 * g:4 * g + 4, :], T.XT_d[:, 4 * g:4 * g + 4, :], [], [("XT", g)])
        conv_weight(P, T.w_out_b[l], T.w_out[l], D, "cw_out")
        if l < T.ffl:
            conv_weight(P, T.w_ff1_b[l], T.w_ff1[l], D, "cw_ff1", step=256)
        WinV = T.w_in_b[l].rearrange("(kc k) c -> k kc c", k=128)
        psr = pring("psA", 4, [128, 512])
        XTR = [("XT", g) for g in range(4)]
        TG = [(0, 512), (512, 512), (1024, 512), (1536, 512), (2048, 64)]
        wfm = ring("wfm", 3, [128, KC, 128], BF16)
        stg = ring("stgfm", 2, [128, LP], BF16)
        fm = [(C_GQ + 128 * h, h, 128.0 ** -0.5) for h in range(4)]
        fm += [(C_GK + 128 * h, 4 + h, 1.0) for h in range(4)]
        fm += [(C_NQ + 128 * j, 8 + j, 0.125) for j in range(8)]
        fm += [(C_NK + 128 * j, 16 + j, 1.0) for j in range(8)]
        ev = 0
        for (c0, blk, scale) in fm:
            wt, wr = wfm.next()
            P.dma(wt[:, :, :], WinV[:, :, c0:c0 + 128], [], [wr])
            sg, sr = stg.next()
            for tg, (t0, n) in enumerate(TG):
                pt, pr = psr.next()
                for kc in range(KC):
                    P.mm(pt[:, 0:n], wt[:, kc, :], XT[:, kc, t0:t0 + n], kc == 0, kc == KC - 1, [wr, ("XT", kc // 4)], [pr])
                if ev % 2 == 0:
                    P.act(sg[:, t0:t0 + n], pt[:, 0:n], AF.Identity, [pr], [(sr, tg)], scale=scale)
                else:
                    P.ts("vector", sg[:, t0:t0 + n], pt[:, 0:n], scale, ALU.mult, [pr], [(sr, tg)])
                ev += 1
            P.dma(T.QK_d[blk], sg[:, :], [(sr, tg) for tg in range(5)], [("QK_d", blk)])
        wlr = sb("wlr", [128, KC, 32], BF16)
        P.dma(wlr[:, :, :], WinV[:, :, C_LR:C_LR + 32], [], ["wlr"])
        slr = ring("slr", 2, [16, LP], F32)
        for z in range(2):
            sg, sr = slr.next()
            for tg, (t0, n) in enumerate(TG):
                pt, pr = psr.next()
                for kc in range(KC):
                    P.mm(pt[0:16, 0:n], wlr[:, kc, 16 * z:16 * z + 16], XT[:, kc, t0:t0 + n], kc == 0, kc == KC - 1, ["wlr", ("XT", kc // 4)], [pr])
                P.copy("vector", sg[:, t0:t0 + n], pt[0:16, 0:n], [pr], [(sr, tg)])
            P.dma(T.LRT_d[z], sg[:, :], [(sr, tg) for tg in range(5)], [("LRT_d", z)])
        wtm = ring("wtm", 2, [128, KC, 512], BF16)
        sb16 = ring("stb", 3, [128, 512], BF16)
        sf32 = ring("stf", 3, [128, 512], F32)
        NWA = sb("NWA", [128, 512], F32)
        P.dma(NWA[:, :], T.gnw[l], [], ["NWA"])
        tmb = [(C_GK, T.TMK, 0, BF16), (C_GV, T.TMV, 0, BF16), (C_GV + 512, T.TMV, 512, BF16),
               (C_GR, T.TMR, 0, F32), (C_GR + 512, T.TMR, 512, F32),
               (C_NV, T.TMNV, 0, BF16), (C_NV + 512, T.TMNV, 512, BF16)]
        for bi, (c0, dst, dc, dt) in enumerate(tmb):
            wt, wr = wtm.next()
            for g in range(2):
                P.dma(wt[:, 8 * g:8 * g + 8, :], WinV[:, 8 * g:8 * g + 8, c0:c0 + 512], [], [(wr, g)])
            for i in range(17):
                M = 128 if i < 16 else 64
                pt, pr = psr.next()
                for kc in range(KC):
                    P.mm(pt[0:M, :], XT[:, kc, 128 * i:128 * i + M], wt[:, kc, :], kc == 0, kc == KC - 1, [(wr, kc // 8), ("XT", kc // 4)], [pr])
                sg, sr = (sb16 if dt == BF16 else sf32).next()
                if dt == F32:
                    P.act(sg[0:M, :], pt[0:M, :], AF.Silu, [pr], [sr])
                    P.tt("vector", sg[0:M, :], sg[0:M, :], NWA[0:M, :], ALU.mult, [sr, "NWA"], [sr])
                elif ev % 2 == 0:
                    P.copy("scalar", sg[0:M, :], pt[0:M, :], [pr], [sr])
                else:
                    P.copy("vector", sg[0:M, :], pt[0:M, :], [pr], [sr])
                ev += 1
                P.dma(dst[128 * i:128 * i + M, dc:dc + 512], sg[0:M, :], [sr], [("TM", bi, i)])
        P.emit()


def phaseG(nc, P, T, l):
    with contextlib.ExitStack() as st:
        sb, ps, ring, pring = _alloc(nc, st, f"G{l}_")
        QKG = sb("QKG", [128, 8, LP], BF16)
        for g in range(2):
            P.dma(QKG[:, 4 * g:4 * g + 4, :], T.QK_d[4 * g:4 * g + 4].rearrange("b k t -> k b t"), [], [("QKG", g)])
        if l < T.ffl:
            conv_weight(P, T.w_ff2_b[l], T.w_ff2[l], DFF, "cw_ff2", step=1024)
        if l + 1 < T.wl:
            conv_weight(P, T.w_in_b[l + 1], T.w_in[l + 1], D, "cw_in")
        LRT = []
        WUP = []
        for z in range(2):
            t = sb(f"LRT{z}", [17, LP], F32)
            P.memset("gpsimd", t[:, :], 1.0, [], [("LRT", z)])
            P.dma(t[0:16, :], T.LRT_d[z], [], [("LRT", z)])
            LRT.append(t)
            w = sb(f"WUP{z}", [17, 512], F32)
            P.dma(w[:, :], T.wup[l, z], [], [("WUP", z)])
            WUP.append(w)
        CST = sb("CST", [64, 6, 64], F32)
        P.dma(CST[:, :, :], T.cst, [], ["CST"])
        IDF = sb("IDF", [128, 128], F32)
        P.dma(IDF[:, :], T.ident, [], ["IDF"])
        YG = sb("YG", [128, 8, LP], BF16)
        S = sb("S", [128, 4, 256], F32)
        Sb = ring("Sb", 2, [128, 4, 256], BF16)
        kk = ring("kk", 3, [64, 512], BF16)
        vv = ring("vv", 3, [64, 1024], BF16)
        rr_ = ring("rr", 3, [64, 1024], F32)
        of_ = ring("of", 3, [64, 1024], F32)
        sp_ = ring("sp", 2, [64, 512], F32)
        EB_ = ring("EB", 3, [128, 256], F32)
        ENB_ = ring("ENB", 2, [128, 256], F32)
        ER_ = ring("ER", 2, [64, 512], F32)
        qt_ = ring("qt", 3, [128, 4, 64], BF16)
        kt_ = ring("kt", 2, [128, 4, 64], BF16)
        kh_ = ring("kh", 2, [64, 512], BF16)
        A_ = ring("A", 3, [64, 4, 64], BF16)
        dS_ = ring("dS", 2, [128, 1024], F32)
        y_ = ring("y", 2, [64, 1024], F32)
        sq_ = ring("sq", 2, [64, 256], F32)
        st4_ = ring("st4", 2, [64, 12], F32)
        psA = ps("gA", [128, 512])
        psB = ps("gB", [128, 256])
        psC = ps("gC", [64, 512])
        psD = ps("gD", [64, 256])
        psO = [ps("gO0", [64, 512]), ps("gO1", [64, 512])]
        psS = [ps("gS0", [128, 512]), ps("gS1", [128, 512])]

        def prep(z, c):
            t0 = 64 * c
            X = Ctx()
            X.c, X.t0 = c, t0
            kkt, kkr = kk.next()
            P.dma(kkt[:, :], T.TMK[t0:t0 + 64, :], [], [kkr])
            X.vvt, X.vvr = vv.next()
            P.dma(X.vvt[:, :], T.TMV[t0:t0 + 64, :], [], [X.vvr])
            X.oft, X.ofr = of_.next()
            if z == 1:
                X.rrt, X.rrr = rr_.next()
                P.dma(X.rrt[:, :], T.TMR[t0:t0 + 64, :], [], [X.rrr])
                P.dma(X.oft[:, :], T.OF_d[t0:t0 + 64, :], [], [X.ofr])
            P.mm(psA[0:64, :], LRT[z][0:17, t0:t0 + 64], WUP[z][0:17, :], True, True, [("LRT", z), ("WUP", z)], ["gA"])
            spt, spr = sp_.next()
            P.act(spt[:, :], psA[0:64, :], AF.Exp, ["gA"], [spr], scale=-1.0)
            P.act(spt[:, :], spt[:, :], AF.Ln, [spr], [spr], bias=1.0)
            for h in range(4):
                P.mm(psB[:, 64 * h:64 * h + 64], spt[:, 128 * h:128 * h + 128], CST[:, z, :], True, True, [spr, "CST"], ["gB"])
            P.mm(psC[:, :], CST[:, 2 + z, :], spt[:, :], True, True, [spr, "CST"], ["gC"])
            X.EBt, X.EBr = EB_.next()
            ENBt, ENBr = ENB_.next()
            ERt, ERr = ER_.next()
            P.act(X.EBt[:, :], psB[:, :], AF.Exp, ["gB"], [X.EBr])
            P.act(ENBt[:, :], psB[:, :], AF.Exp, ["gB"], [ENBr], scale=-1.0)
            P.act(ERt[:, :], psC[:, :], AF.Exp, ["gC"], [ERr])
            X.qtt, X.qtr = qt_.next()
            ktt, ktr = kt_.next()
            kht, khr = kh_.next()
            P.tt("vector", X.qtt[:, :, :], QKG[:, 0:4, t0:t0 + 64], X.EBt[:, :].rearrange("p (h t) -> p h t", h=4), ALU.mult, [("QKG", 0), X.EBr], [X.qtr])
            P.tt("vector", ktt[:, :, :], QKG[:, 4:8, t0:t0 + 64], ENBt[:, :].rearrange("p (h t) -> p h t", h=4), ALU.mult, [("QKG", 1), ENBr], [ktr])
            P.tt("vector", kht[:, :], kkt[:, :], ERt[:, :], ALU.mult, [kkr, ERr], [khr])
            for h in range(4):
                P.mm(psD[:, 64 * h:64 * h + 64], ktt[:, h, :], X.qtt[:, h, :], True, True, [ktr, X.qtr], ["gD"])
            X.At, X.Ar = A_.next()
            P.tt("vector", X.At[:, :, :], psD[:, :].rearrange("p (h t) -> p h t", h=4),
                 CST[:, 4 + z, :].unsqueeze(1).to_broadcast([64, 4, 64]), ALU.mult, ["gD", "CST"], [X.Ar])
            for h in range(4):
                P.mm(psS[h // 2][:, 256 * (h % 2):256 * (h % 2) + 256], kht[:, 128 * h:128 * h + 128], X.vvt[:, 256 * h:256 * h + 256],
                     True, True, [khr, X.vvr], [("gS", h // 2)])
            X.dSt, X.dSr = dS_.next()
            for j in range(2):
                P.copy("scalar", X.dSt[:, 512 * j:512 * j + 512], psS[j][:, :], [("gS", j)], [(X.dSr, j)])
            return X

        state = {}

        def finish(z, X):
            c, t0 = X.c, X.t0
            sbt_old, sbr_old = state["sb"]
            for h in range(4):
                o_ap = psO[h // 2][:, 256 * (h % 2):256 * (h % 2) + 256]
                P.mm(o_ap, X.At[:, h, :], X.vvt[:, 256 * h:256 * h + 256], True, False, [X.Ar, X.vvr], [("gO", h // 2)])
                P.mm(o_ap, X.qtt[:, h, :], sbt_old[:, h, :], False, True, [X.qtr, sbr_old], [("gO", h // 2)])
            for h in range(4):
                col = 64 * h + (63 if z == 0 else 0)
                P.stt("vector", S[:, h, :], S[:, h, :], X.EBt[:, col:col + 1], X.dSt[:, 256 * h:256 * h + 256],
                      ALU.mult, ALU.add, [("S", h), X.EBr, (X.dSr, h // 2)], [("S", h)])
            sbt, sbr = Sb.next()
            P.copy("scalar", sbt[:, :, :], S[:, :, :], [("S", h) for h in range(4)], [sbr])
            state["sb"] = (sbt, sbr)
            oft, ofr = X.oft, X.ofr
            if z == 0:
                for j in range(2):
                    P.copy("scalar", oft[:, 512 * j:512 * j + 512], psO[j][:, :], [("gO", j)], [(ofr, j)])
                P.dma(T.OF_d[t0:t0 + 64, :], oft[:, :], [(ofr, 0), (ofr, 1)], [("OF_d", c)])
                return
            rrt, rrr = X.rrt, X.rrr
            for j in range(2):
                P.tt("vector", oft[:, 512 * j:512 * j + 512], oft[:, 512 * j:512 * j + 512], psO[j][:, :], ALU.add, [ofr, ("gO", j)], [ofr])
            sqt, sqr = sq_.next()
            s4, s4r = st4_.next()
            P.memset("gpsimd", s4[:, 0:4], 0.0, [], [(s4r, "acc")])
            for h in range(4):
                P.act(sqt[:, :], oft[:, 256 * h:256 * h + 256], AF.Square, [ofr, (s4r, "acc")], [sqr, (s4r, "acc", h)], accum_out=s4[:, h:h + 1])
            P.act(s4[:, 4:8], s4[:, 0:4], AF.Ln, [sqr, (s4r, "acc")] + [(s4r, "acc", h) for h in range(4)], [s4r], bias=EPS, scale=1.0 / 256.0)
            P.act(s4[:, 8:12], s4[:, 4:8], AF.Exp, [s4r], [s4r], scale=-0.5)
            gt, gr = rrt, rrr
            yt, yr = y_.next()
            for h in range(4):
                P.stt("vector", yt[:, 256 * h:256 * h + 256], oft[:, 256 * h:256 * h + 256], s4[:, 8 + h:9 + h],
                      gt[:, 256 * h:256 * h + 256], ALU.mult, ALU.mult, [ofr, s4r, gr], [(yr, h)])
            for j in range(8):
                P.tr(psA[:, 64 * j:64 * j + 64], yt[:, 128 * j:128 * j + 128], IDF[0:64, 0:64], [(yr, j // 2), "IDF"], ["gA"])
            P.copy("scalar", YG[:, :, t0:t0 + 64], psA[:, :].rearrange("p (j t) -> p j t", j=8), ["gA"], [("YG", c)])

        for z in range(2):
            for h in range(4):
                P.memset("vector", S[:, h, :], 0.0, [], [("S", h)])
            sbt, sbr = Sb.next()
            P.memset("vector", sbt[:, :, :], 0.0, [], [sbr])
            state["sb"] = (sbt, sbr)
            order = list(range(NCH)) if z == 0 else list(range(NCH - 1, -1, -1))
            X = prep(z, order[0])
            for i in range(NCH):
                Xn = prep(z, order[i + 1]) if i + 1 < NCH else None
                finish(z, X)
                X = Xn
        P.dma(T.XT_d[:, 0:8, :], YG[:, :, :], [("YG", c) for c in range(NCH)], ["XTd_g"])
        P.emit()


def phaseN(nc, P, T, l):
    with contextlib.ExitStack() as st:
        sb, ps, ring, pring = _alloc(nc, st, f"N{l}_")
        Wtab = sb("Wtab", [128, 16, 14, 64], F32)
        NM = sb("NM", [128, 64], F32)
        IDF = sb("IDF", [128, 128], F32)
        P.dma(NM[:, :], T.nmask, [], ["NM"])
        P.dma(IDF[:, :], T.ident, [], ["IDF"])
        for j in range(4):
            wv = Wtab[:, 4 * j:4 * j + 4, :, :]
            P.dma(wv, T.bt[l][:, 4 * j:4 * j + 4, :, :], [], [("Wtab", j)])
            P.act(wv, wv, AF.Exp, [("Wtab", j)], [("Wtab", j)])
            wv3 = wv.rearrange("p h m c -> p (h m) c")
            P.tt("vector", wv3, wv3, NM[:, :].unsqueeze(1).to_broadcast([128, 56, 64]), ALU.mult, [("Wtab", j), "NM"], [("Wtab", j)])
        qT_ = ring("nqT", 2, [128, LP], BF16)
        kT_ = ring("nkT", 2, [128, LP], BF16)
        vE_ = ring("nvE", 2, [128, 16, 2, 65], BF16)
        vO_ = ring("nvO", 2, [128, 16, 2, 65], BF16)
        vM_ = ring("nvM", 2, [16, 2, 65], BF16)
        YP_ = ring("nYP", 2, [64, NCH, 128], F32)
        YM_ = ring("nYM", 2, [16, 128], F32)
        YT_ = ring("nYT", 2, [128, LP], BF16)
        E32_ = ring("nE32", 3, [128, 256], F32)
        Eb_ = ring("nEb", 3, [128, 4, 64], BF16)
        EM_ = ring("nEM", 3, [16, 64], BF16)
        rc_ = ring("nrc", 4, [64, 1], F32)
        for rg in (vE_, vO_):
            for k, t in enumerate(rg.tiles):
                P.memset("gpsimd", t[:, :, :, :], 1.0, [], [(rg.name, k)])
        for k, t in enumerate(vM_.tiles):
            P.memset("gpsimd", t[:, :, :], 1.0, [], [("nvM", k)])
        pS_ = pring("nS", 3, [128, 512])
        pM_ = pring("nM", 2, [64, 512])
        pO_ = pring("nO", 2, [64, 512])
        pT_ = pring("nT", 1, [128, 512])
        for hp in range(8):
            qT, qr = qT_.next()
            kT, kr_ = kT_.next()
            P.dma(qT[:, :], T.QK_d[8 + hp], [], [qr])
            P.dma(kT[:, :], T.QK_d[16 + hp], [], [kr_])
            vE, ver = vE_.next()
            vO, vor = vO_.next()
            vM, vmr = vM_.next()
            for hh in range(2):
                cs_ = 128 * hp + 64 * hh
                P.dma(vE[:, :, hh, 0:64], T.TMNV[0:2048, cs_:cs_ + 64].rearrange("(j p) d -> p j d", p=128), [], [ver])
                P.dma(vO[:, :, hh, 0:64], T.TMNV[64:2112, cs_:cs_ + 64].rearrange("(j p) d -> p j d", p=128), [], [vor])
                P.dma(vM[:, hh, 0:64], T.TMNV[48:64, cs_:cs_ + 64], [], [vmr])
            YP, ypr = YP_.next()
            YM, ymr = YM_.next()
            YT, ytr = YT_.next()
            P.memset("gpsimd", YT[:, 0:48], 0.0, [], [(ytr, "pad")])
            for hh in range(2):
                po = 64 * hh
                pm, pmr = pM_.next()
                P.mm(pm[0:16, 0:16], kT[po:po + 64, 48:64], qT[po:po + 64, 48:64], True, True, [kr_, qr], [pmr])
                em, emr = EM_.next()
                P.act(em[0:16, 0:16], pm[0:16, 0:16], AF.Exp, [pmr], [emr])
                po_, por = pO_.next()
                P.mm(po_[0:16, 0:65], em[0:16, 0:16], vM[0:16, hh, :], True, True, [emr, vmr], [por])
                rc, rcr = rc_.next()
                P.add("vector", (lambda e, a=rc, b=po_: e.reciprocal(a[0:16, :], b[0:16, 64:65])), [por], [rcr])
                P.ts("vector", YM[0:16, 64 * hh:64 * hh + 64], po_[0:16, 0:64], rc[0:16, 0:1], ALU.mult, [por, rcr], [(ymr, hh)])
            for r in range(32):
                c = r + 1
                rs = min(max(r - 4, 0), 24)
                dl = r - rs
                cf = rs + 1
                if cf % 2 == 0:
                    vX, vxr, j0 = vE, ver, cf // 2
                else:
                    vX, vxr, j0 = vO, vor, (cf - 1) // 2
                m0 = 7 - dl
                for hh in range(2):
                    h = 2 * hp + hh
                    po = 64 * hh
                    qsl = qT[po:po + 64, 64 * c:64 * c + 64]
                    pS, psr = pS_.next()
                    for j in range(4):
                        k0 = 64 * cf + 128 * j
                        P.mm(pS[:, 64 * j:64 * j + 64], kT[po:po + 64, k0:k0 + 128], qsl, True, True, [kr_, qr], [psr])
                    pm, pmr = pM_.next()
                    P.mm(pm[0:16, 0:64], kT[po:po + 64, 48:64], qsl, True, True, [kr_, qr], [pmr])
                    e32, e32r = E32_.next()
                    P.act(e32[:, :], pS[:, 0:256], AF.Exp, [psr], [e32r])
                    em, emr = EM_.next()
                    P.act(em[:, :], pm[0:16, 0:64], AF.Exp, [pmr], [emr])
                    eb, ebr = Eb_.next()
                    P.tt("vector", eb[:, :, :], e32[:, :].rearrange("p (k c) -> p k c", k=4), Wtab[:, h, m0:m0 + 7:2, :], ALU.mult,
                         [e32r, ("Wtab", h // 4)], [ebr])
                    po_, por = pO_.next()
                    for j in range(4):
                        P.mm(po_[:, 0:65], eb[:, j, :], vX[:, j0 + j, hh, :], j == 0, False, [ebr, vxr], [por])
                    P.mm(po_[:, 0:65], em[:, :], vM[:, hh, :], False, True, [emr, vmr], [por])
                    rc, rcr = rc_.next()
                    P.add("vector", (lambda e, a=rc, b=po_: e.reciprocal(a[:, :], b[:, 64:65])), [por], [rcr])
                    P.ts("vector", YP[:, c, 64 * hh:64 * hh + 64], po_[:, 0:64], rc[:, 0:1], ALU.mult, [por, rcr], [(ypr, c, hh)])
            pt, ptr = pT_.next()
            P.tr(pt[:, 0:16], YM[0:16, :], IDF[0:16, 0:16], [(ymr, 0), (ymr, 1), "IDF"], [ptr])
            P.copy("scalar", YT[:, 48:64], pt[:, 0:16], [ptr], [(ytr, "m")])
            for g in range(4):
                pt, ptr = pT_.next()
                for j in range(8):
                    c = 1 + 8 * g + j
                    P.tr(pt[:, 64 * j:64 * j + 64], YP[:, c, :], IDF[0:64, 0:64], [(ypr, c, 0), (ypr, c, 1), "IDF"], [ptr])
                P.copy("scalar", YT[:, 64 + 512 * g:64 + 512 * g + 512], pt[:, :], [ptr], [(ytr, g)])
            P.dma(T.XT_d[:, 8 + hp, :], YT[:, :], [(ytr, "pad"), (ytr, "m")] + [(ytr, g) for g in range(4)], [("XTd_n", hp)])
        P.emit()


def layer_norm_tail(P, z, zr, M, stats, mv, sm, LNW, LNB, smr):
    P.add("vector", (lambda e: e.bn_aggr(mv[0:M, :], stats[0:M, :, :])), [zr], [smr])
    P.act(sm[0:M, 0:1], mv[0:M, 1:2], AF.Ln, [smr], [smr], bias=EPS, scale=1.0)
    P.act(sm[0:M, 1:2], sm[0:M, 0:1], AF.Exp, [smr], [smr], scale=-0.5)
    P.stt("vector", sm[0:M, 2:3], mv[0:M, 0:1], -1.0, sm[0:M, 1:2], ALU.mult, ALU.mult, [smr], [smr])
    P.act(z[0:M, :], z[0:M, :], AF.Identity, [zr, smr], [zr], bias=sm[0:M, 2:3], scale=sm[0:M, 1:2])
    P.tt("gpsimd", z[0:M, :], z[0:M, :], LNW[0:M, :], ALU.mult, [zr, "LNW"], [zr])
    P.tt("gpsimd", z[0:M, :], z[0:M, :], LNB[0:M, :], ALU.add, [zr, "LNB"], [zr])


def phaseC(nc, P, T, l):
    with contextlib.ExitStack() as st:
        sb, ps, ring, pring = _alloc(nc, st, f"C{l}_")
        WO = sb("WO", [128, KC, D], BF16)
        WoV = T.w_out_b[l].rearrange("(kc k) c -> k kc c", k=128)
        for g in range(4):
            P.dma(WO[:, 4 * g:4 * g + 4, :], WoV[:, 4 * g:4 * g + 4, :], [], [("WO", g)])
        LNW = sb("LNW", [128, D], F32)
        LNB = sb("LNB", [128, D], F32)
        IDB = sb("IDB", [128, 128], BF16)
        P.dma(LNW[:, :], T.ln1w[l], [], ["LNW"])
        P.dma(LNB[:, :], T.ln1b[l], [], ["LNB"])
        P.dma(IDB[:, :], T.ident, [], ["IDB"], q="gpsimd")
        mt_ = ring("cmt", 2, [128, KC, 128], BF16)
        hr_ = ring("chr", 2, [128, D], F32)
        z_ = ring("cz", 2, [128, D], F32)
        zb_ = ring("czb", 2, [128, D], BF16)
        stg_ = ring("cstg", 2, [128, KC, 128], BF16)
        stats_ = ring("cstat", 2, [128, 4, 6], F32)
        mv_ = ring("cmv", 2, [128, 2], F32)
        sm_ = ring("csm", 2, [128, 4], F32)
        pz_ = pring("cpz", 6, [128, 512])
        pT_ = pring("cpT", 1, [128, KC, 128], BF16)
        for i in range(17):
            M = 128 if i < 16 else 64
            mt, mr = mt_.next()
            P.dma(mt[:, :, 0:M], T.XT_d[:, :, 128 * i:128 * i + M], [("XTd", i)], [mr])
            hr, hrr = hr_.next()
            P.dma(hr[0:M, :], T.Hres[128 * i:128 * i + M, :], [("Hres", i)], [hrr])
            z, zr = z_.next()
            stats, sr_ = stats_.next()
            for cb in range(4):
                pz, pzr = pz_.next()
                for kc in range(KC):
                    P.mm(pz[0:M, :], mt[:, kc, 0:M], WO[:, kc, 512 * cb:512 * cb + 512], kc == 0, kc == KC - 1, [mr, ("WO", kc // 4)], [pzr])
                zs = z[0:M, 512 * cb:512 * cb + 512]
                P.stt("vector", zs, hr[0:M, 512 * cb:512 * cb + 512], ALPHA, pz[0:M, :], ALU.mult, ALU.add, [hrr, pzr], [zr])
                P.add("vector", (lambda e, a=stats, b=zs, cb=cb, M=M: e.bn_stats(a[0:M, cb, :], b)), [zr], [zr])
            mv, mvr = mv_.next()
            sm, smr = sm_.next()
            layer_norm_tail(P, z, zr, M, stats, mv, sm, LNW, LNB, smr)
            P.dma(T.Hres[128 * i:128 * i + M, :], z[0:M, :], [zr], [("Hres", i)])
            zb, zbr = zb_.next()
            P.copy("scalar", zb[0:M, :], z[0:M, :], [zr], [zbr])
            pT, pTr = pT_.next()
            for kc in range(KC):
                P.tr(pT[:, kc, 0:M], zb[0:M, 128 * kc:128 * kc + 128], IDB[0:M, 0:M], [zbr, "IDB"], [pTr])
            sg, sgr = stg_.next()
            P.copy("vector", sg[:, :, 0:M], pT[:, :, 0:M], [pTr], [sgr])
            P.dma(T.XT_d[:, :, 128 * i:128 * i + M], sg[:, :, 0:M], [sgr], [("XTd", i)])
        P.emit()


def phaseD(nc, P, T, l, last):
    with contextlib.ExitStack() as st:
        sb, ps, ring, pring = _alloc(nc, st, f"D{l}_")
        LNW = sb("LNW", [128, D], F32)
        LNB = sb("LNB", [128, D], F32)
        IDB = sb("IDB", [128, 128], BF16)
        IDF = sb("IDF", [128, 128], F32)
        P.dma(LNW[:, :], T.ln2w[l], [], ["LNW"])
        P.dma(LNB[:, :], T.ln2b[l], [], ["LNB"])
        P.dma(IDB[:, :], T.ident, [], ["IDB"], q="gpsimd")
        P.dma(IDF[:, :], T.ident, [], ["IDF"])
        NT = 448
        xt_ = ring("dxt", 1, [128, KC, NT], BF16)
        uT = sb("duT", [128, 64, NT], BF16)
        w1_ = ring("dw1", 2, [128, KC, 256], BF16)
        w2_ = ring("dw2", 3, [128, 32, 128], BF16)
        Y2T = sb("dY2T", [128, KC, NT], F32)
        rl_ = ring("drl", 2, [128, NT], F32)
        hr_ = ring("dhr", 1, [128, D], F32)
        z_ = ring("dz", 2, [128, D], F32)
        zb_ = ring("dzb", 1, [128, D], BF16)
        stg_ = ring("dstg", 1, [128, KC, 128], BF16)
        stats_ = ring("dstat", 2, [128, 4, 6], F32)
        mv_ = ring("dmv", 2, [128, 2], F32)
        sm_ = ring("dsm", 2, [128, 4], F32)
        pm_ = pring("dpm", 3, [128, 512])
        pz_ = pring("dpz", 3, [128, 512])
        pT_ = pring("dpT", 1, [128, KC, 128], BF16)
        W1v = T.w_ff1_b[l].rearrange("(kc k) f -> k kc f", k=128)
        W2v = T.w_ff2_b[l].rearrange("(fc f) d -> f fc d", f=128)
        TGS = [(0, 448), (448, 448), (896, 448), (1344, 384), (1728, 384)]
        ev = 0
        for (t0, n) in TGS:
            xt, xr = xt_.next()
            P.dma(xt[:, :, 0:n], T.XT_d[:, :, t0:t0 + n], [("XTd", t0 + 128 * s) for s in range((n + 127) // 128)], [xr])
            for fb2 in range(32):
                w1, w1r = w1_.next()
                P.dma(w1[:, :, :], W1v[:, :, 256 * fb2:256 * fb2 + 256], [], [w1r])
                for j in range(2):
                    fb = 2 * fb2 + j
                    pm, pmr = pm_.next()
                    for kc in range(KC):
                        P.mm(pm[:, 0:n], w1[:, kc, 128 * j:128 * j + 128], xt[:, kc, 0:n], kc == 0, kc == KC - 1, [w1r, xr], [pmr])
                    rl, rlr = rl_.next()
                    P.act(rl[:, 0:n], pm[:, 0:n], AF.Relu, [pmr], [rlr])
                    P.tt("vector" if ev % 2 == 0 else "gpsimd", uT[:, fb, 0:n], rl[:, 0:n], rl[:, 0:n], ALU.mult, [rlr], [("uT", fb)])
                    ev += 1
            for db in range(16):
                pm, pmr = pm_.next()
                for half in range(2):
                    w2, w2r = w2_.next()
                    P.dma(w2[:, :, :], W2v[:, 32 * half:32 * half + 32, 128 * db:128 * db + 128], [], [w2r])
                    for f in range(32):
                        fc = 32 * half + f
                        P.mm(pm[:, 0:n], w2[:, f, :], uT[:, fc, 0:n], fc == 0, fc == 63, [w2r, ("uT", fc)], [pmr])
                if db % 2 == 0:
                    P.copy("scalar", Y2T[:, db, 0:n], pm[:, 0:n], [pmr], [("Y2T", db)])
                else:
                    P.copy("vector", Y2T[:, db, 0:n], pm[:, 0:n], [pmr], [("Y2T", db)])
            for s in range((n + 127) // 128):
                m = min(128, n - 128 * s)
                p0 = t0 + 128 * s
                hr, hrr = hr_.next()
                P.dma(hr[0:m, :], T.Hres[p0:p0 + m, :], [("Hres", p0)], [hrr])
                z, zr = z_.next()
                stats, sr_ = stats_.next()
                for cb in range(4):
                    pz, pzr = pz_.next()
                    for dj in range(4):
                        db = 4 * cb + dj
                        P.tr(pz[0:m, 128 * dj:128 * dj + 128], Y2T[:, db, 128 * s:128 * s + m], IDF[:, :], [("Y2T", db), "IDF"], [pzr])
                    zs = z[0:m, 512 * cb:512 * cb + 512]
                    P.stt("vector", zs, hr[0:m, 512 * cb:512 * cb + 512], ALPHA, pz[0:m, :], ALU.mult, ALU.add, [hrr, pzr], [zr])
                    P.add("vector", (lambda e, a=stats, b=zs, cb=cb, m=m: e.bn_stats(a[0:m, cb, :], b)), [zr], [zr])
                mv, mvr = mv_.next()
                sm, smr = sm_.next()
                layer_norm_tail(P, z, zr, m, stats, mv, sm, LNW, LNB, smr)
                if last:
                    r0 = max(p0, 64)
                    if r0 < p0 + m:
                        P.dma(T.out[r0 - 64:p0 + m - 64, :], z[r0 - p0:m, :], [zr], [("out", p0)])
                    continue
                P.dma(T.Hres[p0:p0 + m, :], z[0:m, :], [zr], [("Hres", p0)])
                zb, zbr = zb_.next()
                P.copy("scalar", zb[0:m, :], z[0:m, :], [zr], [zbr])
                pT, pTr = pT_.next()
                for kc in range(KC):
                    P.tr(pT[:, kc, 0:m], zb[0:m, 128 * kc:128 * kc + 128], IDB[0:m, 0:m], [zbr, "IDB"], [pTr])
                sg, sgr = stg_.next()
                P.copy("vector", sg[:, :, 0:m], pT[:, :, 0:m], [pTr], [sgr])
                if p0 == 0:
                    P.memset("vector", sg[:, :, 0:48], 0.0, [sgr], [sgr])
                P.dma(T.XT_d[:, :, p0:p0 + m], sg[:, :, 0:m], [sgr], [("XTd", p0)])
        P.emit()


def build_program(upto=None, dbg=(), wl=DEPTH, ffl=DEPTH):
    nc = bass.Bass("TRN2", target_bir_lowering=False)
    T = Ctx()

    def din(name, shape, dt=F32):
        return nc.dram_tensor(name, list(shape), dt, kind="ExternalInput").ap()

    def dscr(name, shape, dt):
        kind = "ExternalOutput" if name in dbg else "Internal"
        return nc.dram_tensor(name, list(shape), dt, kind=kind).ap()

    T.x = din("x", [2048, D])
    T.meta = din("meta", [16, D])
    T.wl, T.ffl = wl, ffl
    T.w_in = din("w_in", [wl, D, PT])
    T.w_out = din("w_out", [wl, D, D])
    T.w_ff1 = din("w_ff1", [ffl, D, DFF]) if ffl else None
    T.w_ff2 = din("w_ff2", [ffl, DFF, D]) if ffl else None
    T.wup = din("wup", [DEPTH, 2, 17, 512])
    T.gnw = din("gnw", [DEPTH, 128, 512])
    T.bt = din("bt", [DEPTH, 128, 16, 14, 64])
    T.nmask = din("nmask", [128, 64])
    T.cst = din("cst", [64, 6, 64])
    T.ident = din("ident", [128, 128])
    T.ln1w = din("ln1w", [DEPTH, 128, D])
    T.ln1b = din("ln1b", [DEPTH, 128, D])
    T.ln2w = din("ln2w", [DEPTH, 128, D])
    T.ln2b = din("ln2b", [DEPTH, 128, D])
    T.out = nc.dram_tensor("out", [2048, D], F32, kind="ExternalOutput").ap()
    T.Hres = dscr("Hres", [LP, D], F32)
    T.XT_d = dscr("XT_d", [128, KC, LP], BF16)
    T.w_in_b = [dscr(f"w_in_b{l}", [D, PT], BF16) for l in range(DEPTH)]
    T.w_out_b = [dscr(f"w_out_b{l}", [D, D], BF16) for l in range(DEPTH)]
    T.w_ff1_b = [dscr(f"w_ff1_b{l}", [D, DFF], BF16) for l in range(DEPTH)]
    T.w_ff2_b = [dscr(f"w_ff2_b{l}", [DFF, D], BF16) for l in range(DEPTH)]
    T.QK_d = dscr("QK_d", [24, 128, LP], BF16)
    T.LRT_d = dscr("LRT_d", [2, 16, LP], F32)
    T.TMK = dscr("TMK", [LP, 512], BF16)
    T.TMV = dscr("TMV", [LP, 1024], BF16)
    T.TMR = dscr("TMR", [LP, 1024], F32)
    T.TMNV = dscr("TMNV", [LP, 1024], BF16)
    T.OF_d = dscr("OF_d", [LP, 1024], F32)

    phases = [("0", lambda P: phase0(nc, P, T))]
    for l in range(DEPTH):
        phases.append((f"A{l}", lambda P, l=l: phaseA(nc, P, T, l)))
        phases.append((f"G{l}", lambda P, l=l: phaseG(nc, P, T, l)))
        phases.append((f"N{l}", lambda P, l=l: phaseN(nc, P, T, l)))
        phases.append((f"C{l}", lambda P, l=l: phaseC(nc, P, T, l)))
        phases.append((f"D{l}", lambda P, l=l: phaseD(nc, P, T, l, l == DEPTH - 1)))
    with contextlib.ExitStack() as st:
        P = Prog(nc, st)
        for name, fn in phases:
            fn(P)
            if upto is not None and name == upto:
                break
    return nc


def host_inputs(inputs):
    f32 = np.float32
    s = np.arange(64)
    le = (s[:, None] <= s[None, :]).astype(f32)
    ge = (s[:, None] >= s[None, :]).astype(f32)
    gt = (s[:, None] > s[None, :]).astype(f32)
    lt = (s[:, None] < s[None, :]).astype(f32)
    g = f32(-1.0 / 16.0)
    cst = np.stack([le * g, ge * g, gt * g, lt * g, le, ge], axis=1).astype(f32)
    kc = np.arange(64)[:, None]
    c = np.arange(64)[None, :]
    cs = np.clip(c - 8, 0, 48)
    nmask = ((kc >= cs) & (kc < cs + 16)).astype(f32)
    dc = np.clip(kc - c, -15, 15) + 15
    rb = np.asarray(inputs["na_rel_bias"], f32)
    bt = rb[:, :, :, dc]
    bt = bt.transpose(0, 3, 1, 2, 4)
    bt = np.ascontiguousarray(np.concatenate([bt[:, :, :, 0:14, :], bt[:, :, :, 1:15, :]], axis=1))
    nmask = np.ascontiguousarray(np.concatenate([nmask, nmask], axis=0))
    wup = np.concatenate([np.asarray(inputs["gla_w_up"], f32), np.asarray(inputs["gla_b_up"], f32)[:, :, None, :]], axis=2)
    gnw = np.ascontiguousarray(np.broadcast_to(np.tile(np.asarray(inputs["gla_norm_w"], f32), (1, 2))[:, None, :], (DEPTH, 128, 512)))
    rep = lambda a: np.ascontiguousarray(np.broadcast_to(np.asarray(a, f32)[:, None, :], (DEPTH, 128, D)))
    shared = {
        "meta": np.asarray(inputs["meta"], f32),
        "w_in": np.asarray(inputs["w_in"], f32),
        "w_out": np.asarray(inputs["w_out"], f32),
        "w_ff1": np.asarray(inputs["w_ff1"], f32),
        "w_ff2": np.asarray(inputs["w_ff2"], f32),
        "wup": np.ascontiguousarray(wup),
        "gnw": gnw,
        "bt": bt,
        "nmask": nmask,
        "cst": cst,
        "ident": np.eye(128, dtype=f32),
        "ln1w": rep(inputs["ln1_w"]),
        "ln1b": rep(inputs["ln1_b"]),
        "ln2w": rep(inputs["ln2_w"]),
        "ln2b": rep(inputs["ln2_b"]),
    }
    return shared


def kernel(**inputs):
    x = np.asarray(inputs["x"], np.float32)
    shared = host_inputs(inputs)
    nc = build_program()
    in_maps = []
    for b in range(NCORES):
        m = dict(shared)
        m["x"] = np.ascontiguousarray(x[b])
        in_maps.append(m)
    res = run_bass_kernel_spmd(nc, in_maps, core_ids=list(range(NCORES)))
    return np.stack([np.asarray(r["out"], np.float32) for r in res.results], axis=0)
```

```python
import contextlib
import numpy as np
import concourse.bass as bass
import concourse.mybir as mybir
from concourse.bass_utils import run_bass_kernel_spmd

F32 = mybir.dt.float32
BF16 = mybir.dt.bfloat16
AF = mybir.ActivationFunctionType
ALU = mybir.AluOpType

D = 2048
LP = 2112
NCH = 33
KC = 16
DFF = 8192
PT = 6176
C_GQ, C_GK, C_GV, C_GR, C_LR, C_NQ, C_NK, C_NV = 0, 512, 1024, 2048, 3072, 3104, 4128, 5152
ALPHA = 4.0 ** 0.25
EPS = 1e-5
DEPTH = 2
NCORES = 4

ENGS = ("tensor", "vector", "scalar", "gpsimd", "sync")


class Prog:
    def __init__(self, nc, stack, n_dma=20, sync_same=True):
        self.nc = nc
        self.n_dma = n_dma
        self.sync_same = sync_same
        self.esem = {e: stack.enter_context(nc.semaphore("s_" + e)) for e in ENGS}
        self.dq = ("sync", "gpsimd")
        self.dsem = {q: [stack.enter_context(nc.semaphore(f"d_{q}_{k}")) for k in range(n_dma)] for q in self.dq}
        self.cnt = {e: 0 for e in ENGS}
        self.dma_k = {q: 0 for q in self.dq}
        self.dval = {q: [0] * n_dma for q in self.dq}
        self.ops = []
        self.nops = 0

    def add(self, eng, fn, reads=(), writes=(), dma=False):
        self.ops.append((eng, fn, tuple(reads), tuple(writes), dma))

    def dma(self, out, in_, r, w, q="sync"):
        self.add(q, lambda e: e.dma_start(out=out, in_=in_), r, w, dma=True)

    def mm(self, out, lhsT, rhs, start, stop, r, w):
        self.add("tensor", lambda e: e.matmul(out, lhsT, rhs, start=start, stop=stop), r, w)

    def tr(self, out, in_, ident, r, w):
        self.add("tensor", lambda e: e.transpose(out, in_, ident), r, w)

    def act(self, out, in_, func, r, w, bias=None, scale=None, accum_out=None):
        kw = {}
        if bias is not None:
            kw["bias"] = bias
        if scale is not None:
            kw["scale"] = scale
        if accum_out is not None:
            kw["accum_out"] = accum_out
        self.add("scalar", lambda e: e.activation(out, in_, func, **kw), r, w)

    def copy(self, eng, out, in_, r, w):
        if eng == "scalar":
            self.add(eng, lambda e: e.copy(out, in_), r, w)
        else:
            self.add(eng, lambda e: e.tensor_copy(out, in_), r, w)

    def tt(self, eng, out, in0, in1, op, r, w):
        self.add(eng, lambda e: e.tensor_tensor(out, in0, in1, op), r, w)

    def ts(self, eng, out, in0, s1, op0, r, w, s2=None, op1=None):
        if op1 is None:
            self.add(eng, lambda e: e.tensor_scalar(out, in0, s1, None, op0), r, w)
        else:
            self.add(eng, lambda e: e.tensor_scalar(out, in0, s1, s2, op0, op1), r, w)

    def stt(self, eng, out, in0, scalar, in1, op0, op1, r, w):
        self.add(eng, lambda e: e.scalar_tensor_tensor(out, in0, scalar, in1, op0, op1), r, w)

    def memset(self, eng, ap, val, r, w):
        self.add(eng, lambda e: e.memset(ap, val), r, w)

    def emit(self):
        nc = self.nc
        ops = self.ops
        self.ops = []
        n = len(ops)
        self.nops += n
        last_w = {}
        readers = {}
        deps = [None] * n
        for i, (eng, fn, reads, writes, dma) in enumerate(ops):
            d = set()
            for r in reads:
                j = last_w.get(r)
                if j is not None:
                    d.add(j)
            for w in writes:
                j = last_w.get(w)
                if j is not None:
                    d.add(j)
                rr = readers.get(w)
                if rr:
                    d.update(rr)
            d.discard(i)
            for r in reads:
                readers.setdefault(r, []).append(i)
            for w in writes:
                last_w[w] = i
                readers[w] = []
            deps[i] = d
        signal = [False] * n
        need = [None] * n
        for i in range(n):
            eng_i = ops[i][0]
            dma_i = ops[i][4]
            lst = []
            for j in deps[i]:
                eng_j = ops[j][0]
                dma_j = ops[j][4]
                if not dma_j and not dma_i and eng_j == eng_i:
                    if eng_i == "tensor" or not self.sync_same:
                        continue
                lst.append(j)
                signal[j] = True
            need[i] = lst
        last_on = {}
        for i in range(n):
            if not ops[i][4]:
                last_on[ops[i][0]] = i
        for e, i in last_on.items():
            signal[i] = True
        count = [0] * n
        dslot = [None] * n
        for i in range(n):
            eng, fn, reads, writes, dma = ops[i]
            if dma:
                k = self.dma_k[eng]
                self.dma_k[eng] += 1
                s = k % self.n_dma
                self.dval[eng][s] += 16
                dslot[i] = (eng, s, self.dval[eng][s])
            elif signal[i]:
                self.cnt[eng] += 1
                count[i] = self.cnt[eng]
        fin_cnt = dict(self.cnt)
        fin_dval = {q: list(v) for q, v in self.dval.items()}
        esem, dsem = self.esem, self.dsem

        def make(engname):
            def body(eng):
                waited = {}

                def do_wait(key, sem, val):
                    if waited.get(key, 0) >= val:
                        return
                    waited[key] = val
                    eng.wait_ge(sem, val)

                for i in range(n):
                    e_i, fn, reads, writes, dma = ops[i]
                    if e_i != engname:
                        continue
                    waits = {}
                    for j in need[i]:
                        if ops[j][4]:
                            qe, s, val = dslot[j]
                            key = ("d", qe, s)
                            sem = dsem[qe][s]
                        else:
                            key = ("e", ops[j][0])
                            sem = esem[ops[j][0]]
                            val = count[j]
                        if waits.get(key, (None, 0))[1] < val:
                            waits[key] = (sem, val)
                    if dma:
                        qe, s, val = dslot[i]
                        if val > 16:
                            key = ("d", qe, s)
                            if waits.get(key, (None, 0))[1] < val - 16:
                                waits[key] = (dsem[qe][s], val - 16)
                    for key, (sem, val) in waits.items():
                        do_wait(key, sem, val)
                    ins = fn(eng)
                    if dma:
                        qe, s, val = dslot[i]
                        ins.then_inc(dsem[qe][s], 16)
                    elif signal[i]:
                        ins.then_inc(esem[engname], 1)
                for e in ENGS:
                    if fin_cnt[e] > 0:
                        do_wait(("e", e), esem[e], fin_cnt[e])
                for q in self.dq:
                    for s in range(self.n_dma):
                        if fin_dval[q][s] > 0:
                            do_wait(("d", q, s), dsem[q][s], fin_dval[q][s])

            return body

        with nc.Block() as block:
            block.tensor(make("tensor"))
            block.vector(make("vector"))
            block.scalar(make("scalar"))
            block.gpsimd(make("gpsimd"))
            block.sync(make("sync"))


class Ring:
    def __init__(self, tiles, name):
        self.tiles = tiles
        self.name = name
        self.i = -1

    def next(self):
        self.i += 1
        k = self.i % len(self.tiles)
        return self.tiles[k], (self.name, k)


class Ctx:
    pass


def _alloc(nc, st, pfx=""):
    def sb(name, shape, dt):
        return st.enter_context(nc.sbuf_tensor(pfx + name, list(shape), dt))

    def ps(name, shape, dt=F32):
        return st.enter_context(nc.psum_tensor(pfx + name, list(shape), dt))

    def ring(name, n, shape, dt):
        return Ring([sb(f"{name}{k}", shape, dt) for k in range(n)], name)

    def pring(name, n, shape, dt=F32):
        return Ring([ps(f"{name}{k}", shape, dt) for k in range(n)], name)

    return sb, ps, ring, pring


def conv_weight(P, dst, src, rows, name, step=512):
    for r0 in range(0, rows, step):
        P.dma(dst[r0:r0 + step, :], src[r0:r0 + step, :], [], [(name, r0)], q="gpsimd")


def phase0(nc, P, T):
    import os
    MODE = os.environ.get("DBG0", "all")
    with contextlib.ExitStack() as st:
        sb, ps, ring, pring = _alloc(nc, st, "p0_")
        zero = sb("zero", [48, D], F32)
        idb = sb("idb", [128, 128], BF16)
        P.memset("gpsimd", zero[:, :], 0.0, [], ["zero"])
        P.dma(T.Hres[0:48, :], zero[:, :], ["zero"], ["HresP"], q="gpsimd")
        P.dma(T.Hres[48:64, :], T.meta, [], ["HresM"], q="gpsimd")
        for j in range(4):
            P.dma(T.Hres[64 + 512 * j:64 + 512 * (j + 1), :], T.x[512 * j:512 * (j + 1), :], [], [("HresX", j)], q="gpsimd")
        P.dma(idb[:, :], T.ident, [], ["idb"], q="gpsimd")
        if MODE in ("all", "conv"):
            conv_weight(P, T.w_in_b[0], T.w_in[0], D, "cw_in0")
        if MODE in ("conv", "init"):
            P.emit()
            return
        x32 = ring("x32", 2, [128, D], F32)
        xb = ring("xb", 2, [128, D], BF16)
        pT = pring("pT", 2, [128, KC, 128], BF16)
        stg = ring("stg", 2, [128, KC, 128], BF16)
        tiles = [int(v) for v in os.environ.get("DBGT", ",".join(str(v) for v in range(17))).split(",")]
        for i in tiles:
            M = 128 if i < 16 else 64
            xt, xr = x32.next()
            if i == 0:
                P.memset("vector", xt[0:64, :], 0.0, [], [xr])
                P.dma(xt[48:64, :], T.meta, [xr], [(xr, "m")])
                P.dma(xt[64:128, :], T.x[0:64, :], [], [(xr, "x")])
                rd = [xr, (xr, "m"), (xr, "x")]
            else:
                P.dma(xt[0:M, :], T.x[128 * i - 64:128 * i - 64 + M, :], [], [xr, (xr, "m"), (xr, "x")])
                rd = [xr]
            xbt, xbr = xb.next()
            P.copy("scalar", xbt[0:M, :], xt[0:M, :], rd, [xbr])
            pt, pr = pT.next()
            for kc in range(KC):
                P.tr(pt[:, kc, 0:M], xbt[0:M, 128 * kc:128 * (kc + 1)], idb[0:M, 0:M], [xbr, "idb"], [pr])
            sg, sr = stg.next()
            P.copy("vector", sg[:, :, 0:M], pt[:, :, 0:M], [pr], [sr])
            P.dma(T.XT_d[:, :, 128 * i:128 * i + M], sg[:, :, 0:M], [sr], [("XTd", i)])
        P.emit()


def phaseA(nc, P, T, l):
    with contextlib.ExitStack() as st:
        sb, ps, ring, pring = _alloc(nc, st, f"A{l}_")
        XT = sb("XT", [128, KC, LP], BF16)
        for g in range(4):
            P.dma(XT[:, 4 * g:4 * g + 4, :], T.XT_d[:, 4 * g:4 * g + 4, :], [], [("XT", g)])
        conv_weight(P, T.w_out_b[l], T.w_out[l], D, "cw_out")
        if l < T.ffl:
            conv_weight(P, T.w_ff1_b[l], T.w_ff1[l], D, "cw_ff1", step=256)
        WinV = T.w_in_b[l].rearrange("(kc k) c -> k kc c", k=128)
        psr = pring("psA", 4, [128, 512])
        XTR = [("XT", g) for g in range(4)]
        TG = [(0, 512), (512, 512), (1024, 512), (1536, 512), (2048, 64)]
        wfm = ring("wfm", 3, [128, KC, 128], BF16)
        stg = ring("stgfm", 2, [128, LP], BF16)
        fm = [(C_GQ + 128 * h, h, 128.0 ** -0.5) for h in range(4)]
        fm += [(C_GK + 128 * h, 4 + h, 1.0) for h in range(4)]
        fm += [(C_NQ + 128 * j, 8 + j, 0.125) for j in range(8)]
        fm += [(C_NK + 128 * j, 16 + j, 1.0) for j in range(8)]
        ev = 0
        for (c0, blk, scale) in fm:
            wt, wr = wfm.next()
            P.dma(wt[:, :, :], WinV[:, :, c0:c0 + 128], [], [wr])
            sg, sr = stg.next()
            for tg, (t0, n) in enumerate(TG):
                pt, pr = psr.next()
                for kc in range(KC):
                    P.mm(pt[:, 0:n], wt[:, kc, :], XT[:, kc, t0:t0 + n], kc == 0, kc == KC - 1, [wr, ("XT", kc // 4)], [pr])
                if ev % 2 == 0:
                    P.act(sg[:, t0:t0 + n], pt[:, 0:n], AF.Identity, [pr], [(sr, tg)], scale=scale)
                else:
                    P.ts("vector", sg[:, t0:t0 + n], pt[:, 0:n], scale, ALU.mult, [pr], [(sr, tg)])
                ev += 1
            P.dma(T.QK_d[blk], sg[:, :], [(sr, tg) for tg in range(5)], [("QK_d", blk)])
        wlr = sb("wlr", [128, KC, 32], BF16)
        P.dma(wlr[:, :, :], WinV[:, :, C_LR:C_LR + 32], [], ["wlr"])
        slr = ring("slr", 2, [16, LP], F32)
        for z in range(2):
            sg, sr = slr.next()
            for tg, (t0, n) in enumerate(TG):
                pt, pr = psr.next()
                for kc in range(KC):
                    P.mm(pt[0:16, 0:n], wlr[:, kc, 16 * z:16 * z + 16], XT[:, kc, t0:t0 + n], kc == 0, kc == KC - 1, ["wlr", ("XT", kc // 4)], [pr])
                P.copy("vector", sg[:, t0:t0 + n], pt[0:16, 0:n], [pr], [(sr, tg)])
            P.dma(T.LRT_d[z], sg[:, :], [(sr, tg) for tg in range(5)], [("LRT_d", z)])
        wtm = ring("wtm", 2, [128, KC, 512], BF16)
        sb16 = ring("stb", 3, [128, 512], BF16)
        sf32 = ring("stf", 3, [128, 512], F32)
        NWA = sb("NWA", [128, 512], F32)
        P.dma(NWA[:, :], T.gnw[l], [], ["NWA"])
        tmb = [(C_GK, T.TMK, 0, BF16), (C_GV, T.TMV, 0, BF16), (C_GV + 512, T.TMV, 512, BF16),
               (C_GR, T.TMR, 0, F32), (C_GR + 512, T.TMR, 512, F32),
               (C_NV, T.TMNV, 0, BF16), (C_NV + 512, T.TMNV, 512, BF16)]
        for bi, (c0, dst, dc, dt) in enumerate(tmb):
            wt, wr = wtm.next()
            for g in range(2):
                P.dma(wt[:, 8 * g:8 * g + 8, :], WinV[:, 8 * g:8 * g + 8, c0:c0 + 512], [], [(wr, g)])
            for i in range(17):
                M = 128 if i < 16 else 64
                pt, pr = psr.next()
                for kc in range(KC):
                    P.mm(pt[0:M, :], XT[:, kc, 128 * i:128 * i + M], wt[:, kc, :], kc == 0, kc == KC - 1, [(wr, kc // 8), ("XT", kc // 4)], [pr])
                sg, sr = (sb16 if dt == BF16 else sf32).next()
                if dt == F32:
                    P.act(sg[0:M, :], pt[0:M, :], AF.Silu, [pr], [sr])
                    P.tt("vector", sg[0:M, :], sg[0:M, :], NWA[0:M, :], ALU.mult, [sr, "NWA"], [sr])
                elif ev % 2 == 0:
                    P.copy("scalar", sg[0:M, :], pt[0:M, :], [pr], [sr])
                else:
                    P.copy("vector", sg[0:M, :], pt[0:M, :], [pr], [sr])
                ev += 1
                P.dma(dst[128 * i:128 * i + M, dc:dc + 512], sg[0:M, :], [sr], [("TM", bi, i)])
        P.emit()


def phaseG(nc, P, T, l):
    with contextlib.ExitStack() as st:
        sb, ps, ring, pring = _alloc(nc, st, f"G{l}_")
        QKG = sb("QKG", [128, 8, LP], BF16)
        for g in range(2):
            P.dma(QKG[:, 4 * g:4 * g + 4, :], T.QK_d[4 * g:4 * g + 4].rearrange("b k t -> k b t"), [], [("QKG", g)])
        if l < T.ffl:
            conv_weight(P, T.w_ff2_b[l], T.w_ff2[l], DFF, "cw_ff2", step=1024)
        if l + 1 < T.wl:
            conv_weight(P, T.w_in_b[l + 1], T.w_in[l + 1], D, "cw_in")
        LRT = []
        WUP = []
        for z in range(2):
            t = sb(f"LRT{z}", [17, LP], F32)
            P.memset("gpsimd", t[:, :], 1.0, [], [("LRT", z)])
            P.dma(t[0:16, :], T.LRT_d[z], [], [("LRT", z)])
            LRT.append(t)
            w = sb(f"WUP{z}", [17, 512], F32)
            P.dma(w[:, :], T.wup[l, z], [], [("WUP", z)])
            WUP.append(w)
        CST = sb("CST", [64, 6, 64], F32)
        P.dma(CST[:, :, :], T.cst, [], ["CST"])
        IDF = sb("IDF", [128, 128], F32)
        P.dma(IDF[:, :], T.ident, [], ["IDF"])
        YG = sb("YG", [128, 8, LP], BF16)
        S = sb("S", [128, 4, 256], F32)
        Sb = ring("Sb", 2, [128, 4, 256], BF16)
        kk = ring("kk", 3, [64, 512], BF16)
        vv = ring("vv", 3, [64, 1024], BF16)
        rr_ = ring("rr", 3, [64, 1024], F32)
        of_ = ring("of", 3, [64, 1024], F32)
        sp_ = ring("sp", 2, [64, 512], F32)
        EB_ = ring("EB", 3, [128, 256], F32)
        ENB_ = ring("ENB", 2, [128, 256], F32)
        ER_ = ring("ER", 2, [64, 512], F32)
        qt_ = ring("qt", 3, [128, 4, 64], BF16)
        kt_ = ring("kt", 2, [128, 4, 64], BF16)
        kh_ = ring("kh", 2, [64, 512], BF16)
        A_ = ring("A", 3, [64, 4, 64], BF16)
        dS_ = ring("dS", 2, [128, 1024], F32)
        y_ = ring("y", 2, [64, 1024], F32)
        sq_ = ring("sq", 2, [64, 256], F32)
        st4_ = ring("st4", 2, [64, 12], F32)
        psA = ps("gA", [128, 512])
        psB = ps("gB", [128, 256])
        psC = ps("gC", [64, 512])
        psD = ps("gD", [64, 256])
        psO = [ps("gO0", [64, 512]), ps("gO1", [64, 512])]
        psS = [ps("gS0", [128, 512]), ps("gS1", [128, 512])]

        def prep(z, c):
            t0 = 64 * c
            X = Ctx()
            X.c, X.t0 = c, t0
            kkt, kkr = kk.next()
            P.dma(kkt[:, :], T.TMK[t0:t0 + 64, :], [], [kkr])
            X.vvt, X.vvr = vv.next()
            P.dma(X.vvt[:, :], T.TMV[t0:t0 + 64, :], [], [X.vvr])
            X.oft, X.ofr = of_.next()
            if z == 1:
                X.rrt, X.rrr = rr_.next()
                P.dma(X.rrt[:, :], T.TMR[t0:t0 + 64, :], [], [X.rrr])
                P.dma(X.oft[:, :], T.OF_d[t0:t0 + 64, :], [("OF_d", c)], [X.ofr])
            P.mm(psA[0:64, :], LRT[z][0:17, t0:t0 + 64], WUP[z][0:17, :], True, True, [("LRT", z), ("WUP", z)], ["gA"])
            spt, spr = sp_.next()
            P.act(spt[:, :], psA[0:64, :], AF.Exp, ["gA"], [spr], scale=-1.0)
            P.act(spt[:, :], spt[:, :], AF.Ln, [spr], [spr], bias=1.0)
            for h in range(4):
                P.mm(psB[:, 64 * h:64 * h + 64], spt[:, 128 * h:128 * h + 128], CST[:, z, :], True, True, [spr, "CST"], ["gB"])
            P.mm(psC[:, :], CST[:, 2 + z, :], spt[:, :], True, True, [spr, "CST"], ["gC"])
            X.EBt, X.EBr = EB_.next()
            ENBt, ENBr = ENB_.next()
            ERt, ERr = ER_.next()
            P.act(X.EBt[:, :], psB[:, :], AF.Exp, ["gB"], [X.EBr])
            P.act(ENBt[:, :], psB[:, :], AF.Exp, ["gB"], [ENBr], scale=-1.0)
            P.act(ERt[:, :], psC[:, :], AF.Exp, ["gC"], [ERr])
            X.qtt, X.qtr = qt_.next()
            ktt, ktr = kt_.next()
            kht, khr = kh_.next()
            P.tt("vector", X.qtt[:, :, :], QKG[:, 0:4, t0:t0 + 64], X.EBt[:, :].rearrange("p (h t) -> p h t", h=4), ALU.mult, [("QKG", 0), X.EBr], [X.qtr])
            P.tt("vector", ktt[:, :, :], QKG[:, 4:8, t0:t0 + 64], ENBt[:, :].rearrange("p (h t) -> p h t", h=4), ALU.mult, [("QKG", 1), ENBr], [ktr])
            P.tt("vector", kht[:, :], kkt[:, :], ERt[:, :], ALU.mult, [kkr, ERr], [khr])
            for h in range(4):
                P.mm(psD[:, 64 * h:64 * h + 64], ktt[:, h, :], X.qtt[:, h, :], True, True, [ktr, X.qtr], ["gD"])
            X.At, X.Ar = A_.next()
            P.tt("vector", X.At[:, :, :], psD[:, :].rearrange("p (h t) -> p h t", h=4),
                 CST[:, 4 + z, :].unsqueeze(1).to_broadcast([64, 4, 64]), ALU.mult, ["gD", "CST"], [X.Ar])
            for h in range(4):
                P.mm(psS[h // 2][:, 256 * (h % 2):256 * (h % 2) + 256], kht[:, 128 * h:128 * h + 128], X.vvt[:, 256 * h:256 * h + 256],
                     True, True, [khr, X.vvr], [("gS", h // 2)])
            X.dSt, X.dSr = dS_.next()
            for j in range(2):
                P.copy("scalar", X.dSt[:, 512 * j:512 * j + 512], psS[j][:, :], [("gS", j)], [(X.dSr, j)])
            return X

        state = {}

        def finish(z, X):
            c, t0 = X.c, X.t0
            sbt_old, sbr_old = state["sb"]
            for h in range(4):
                o_ap = psO[h // 2][:, 256 * (h % 2):256 * (h % 2) + 256]
                P.mm(o_ap, X.At[:, h, :], X.vvt[:, 256 * h:256 * h + 256], True, False, [X.Ar, X.vvr], [("gO", h // 2)])
                P.mm(o_ap, X.qtt[:, h, :], sbt_old[:, h, :], False, True, [X.qtr, sbr_old], [("gO", h // 2)])
            for h in range(4):
                col = 64 * h + (63 if z == 0 else 0)
                P.stt("vector", S[:, h, :], S[:, h, :], X.EBt[:, col:col + 1], X.dSt[:, 256 * h:256 * h + 256],
                      ALU.mult, ALU.add, [("S", h), X.EBr, (X.dSr, h // 2)], [("S", h)])
            sbt, sbr = Sb.next()
            P.copy("scalar", sbt[:, :, :], S[:, :, :], [("S", h) for h in range(4)], [sbr])
            state["sb"] = (sbt, sbr)
            oft, ofr = X.oft, X.ofr
            if z == 0:
                for j in range(2):
                    P.copy("scalar", oft[:, 512 * j:512 * j + 512], psO[j][:, :], [("gO", j)], [ofr])
                P.dma(T.OF_d[t0:t0 + 64, :], oft[:, :], [ofr], [("OF_d", c)])
                return
            rrt, rrr = X.rrt, X.rrr
            for j in range(2):
                P.tt("vector", oft[:, 512 * j:512 * j + 512], oft[:, 512 * j:512 * j + 512], psO[j][:, :], ALU.add, [ofr, ("gO", j)], [ofr])
            sqt, sqr = sq_.next()
            s4, s4r = st4_.next()
            P.memset("gpsimd", s4[:, 0:4], 0.0, [], [(s4r, "acc")])
            for h in range(4):
                P.act(sqt[:, :], oft[:, 256 * h:256 * h + 256], AF.Square, [ofr, (s4r, "acc")], [sqr, (s4r, "acc", h)], accum_out=s4[:, h:h + 1])
            P.act(s4[:, 4:8], s4[:, 0:4], AF.Ln, [sqr, (s4r, "acc")] + [(s4r, "acc", h) for h in range(4)], [s4r], bias=EPS, scale=1.0 / 256.0)
            P.act(s4[:, 8:12], s4[:, 4:8], AF.Exp, [s4r], [s4r], scale=-0.5)
            gt, gr = rrt, rrr
            yt, yr = y_.next()
            for h in range(4):
                P.stt("vector", yt[:, 256 * h:256 * h + 256], oft[:, 256 * h:256 * h + 256], s4[:, 8 + h:9 + h],
                      gt[:, 256 * h:256 * h + 256], ALU.mult, ALU.mult, [ofr, s4r, gr], [(yr, h)])
            for j in range(8):
                P.tr(psA[:, 64 * j:64 * j + 64], yt[:, 128 * j:128 * j + 128], IDF[0:64, 0:64], [(yr, j // 2), "IDF"], ["gA"])
            P.copy("scalar", YG[:, :, t0:t0 + 64], psA[:, :].rearrange("p (j t) -> p j t", j=8), ["gA"], [("YG", c)])

        for z in range(2):
            for h in range(4):
                P.memset("vector", S[:, h, :], 0.0, [], [("S", h)])
            sbt, sbr = Sb.next()
            P.memset("vector", sbt[:, :, :], 0.0, [], [sbr])
            state["sb"] = (sbt, sbr)
            order = list(range(NCH)) if z == 0 else list(range(NCH - 1, -1, -1))
            X = prep(z, order[0])
            for i in range(NCH):
                Xn = prep(z, order[i + 1]) if i + 1 < NCH else None
                finish(z, X)
                X = Xn
        P.dma(T.XT_d[:, 0:8, :], YG[:, :, :], [("YG", c) for c in range(NCH)], ["XTd_g"])
        P.emit()


def phaseN(nc, P, T, l):
    with contextlib.ExitStack() as st:
        sb, ps, ring, pring = _alloc(nc, st, f"N{l}_")
        Wtab = sb("Wtab", [128, 16, 14, 64], F32)
        NM = sb("NM", [128, 64], F32)
        IDF = sb("IDF", [128, 128], F32)
        P.dma(NM[:, :], T.nmask, [], ["NM"])
        P.dma(IDF[:, :], T.ident, [], ["IDF"])
        for j in range(4):
            wv = Wtab[:, 4 * j:4 * j + 4, :, :]
            P.dma(wv, T.bt[l][:, 4 * j:4 * j + 4, :, :], [], [("Wtab", j)])
            P.act(wv, wv, AF.Exp, [("Wtab", j)], [("Wtab", j)])
            wv3 = wv.rearrange("p h m c -> p (h m) c")
            P.tt("vector", wv3, wv3, NM[:, :].unsqueeze(1).to_broadcast([128, 56, 64]), ALU.mult, [("Wtab", j), "NM"], [("Wtab", j)])
        qT_ = ring("nqT", 2, [128, LP], BF16)
        kT_ = ring("nkT", 2, [128, LP], BF16)
        vE_ = ring("nvE", 2, [128, 16, 2, 65], BF16)
        vO_ = ring("nvO", 2, [128, 16, 2, 65], BF16)
        vM_ = ring("nvM", 2, [16, 2, 65], BF16)
        YP_ = ring("nYP", 2, [64, NCH, 128], F32)
        YM_ = ring("nYM", 2, [16, 128], F32)
        YT_ = ring("nYT", 2, [128, LP], BF16)
        E32_ = ring("nE32", 3, [128, 256], F32)
        Eb_ = ring("nEb", 3, [128, 4, 64], BF16)
        EM_ = ring("nEM", 3, [16, 64], BF16)
        rc_ = ring("nrc", 4, [64, 1], F32)
        for rg in (vE_, vO_):
            for k, t in enumerate(rg.tiles):
                P.memset("gpsimd", t[:, :, :, :], 1.0, [], [(rg.name, k)])
        for k, t in enumerate(vM_.tiles):
            P.memset("gpsimd", t[:, :, :], 1.0, [], [("nvM", k)])
        pS_ = pring("nS", 3, [128, 512])
        pM_ = pring("nM", 2, [64, 512])
        pO_ = pring("nO", 2, [64, 512])
        pT_ = pring("nT", 1, [128, 512])
        for hp in range(8):
            qT, qr = qT_.next()
            kT, kr_ = kT_.next()
            P.dma(qT[:, :], T.QK_d[8 + hp], [], [qr])
            P.dma(kT[:, :], T.QK_d[16 + hp], [], [kr_])
            vE, ver = vE_.next()
            vO, vor = vO_.next()
            vM, vmr = vM_.next()
            for hh in range(2):
                cs_ = 128 * hp + 64 * hh
                P.dma(vE[:, :, hh, 0:64], T.TMNV[0:2048, cs_:cs_ + 64].rearrange("(j p) d -> p j d", p=128), [], [ver])
                P.dma(vO[:, :, hh, 0:64], T.TMNV[64:2112, cs_:cs_ + 64].rearrange("(j p) d -> p j d", p=128), [], [vor])
                P.dma(vM[:, hh, 0:64], T.TMNV[48:64, cs_:cs_ + 64], [], [vmr])
            YP, ypr = YP_.next()
            YM, ymr = YM_.next()
            YT, ytr = YT_.next()
            P.memset("gpsimd", YT[:, 0:48], 0.0, [], [(ytr, "pad")])
            for hh in range(2):
                po = 64 * hh
                pm, pmr = pM_.next()
                P.mm(pm[0:16, 0:16], kT[po:po + 64, 48:64], qT[po:po + 64, 48:64], True, True, [kr_, qr], [pmr])
                em, emr = EM_.next()
                P.act(em[0:16, 0:16], pm[0:16, 0:16], AF.Exp, [pmr], [emr])
                po_, por = pO_.next()
                P.mm(po_[0:16, 0:65], em[0:16, 0:16], vM[0:16, hh, :], True, True, [emr, vmr], [por])
                rc, rcr = rc_.next()
                P.add("vector", (lambda e, a=rc, b=po_: e.reciprocal(a[0:16, :], b[0:16, 64:65])), [por], [rcr])
                P.ts("vector", YM[0:16, 64 * hh:64 * hh + 64], po_[0:16, 0:64], rc[0:16, 0:1], ALU.mult, [por, rcr], [(ymr, hh)])
            for r in range(32):
                c = r + 1
                rs = min(max(r - 4, 0), 24)
                dl = r - rs
                cf = rs + 1
                if cf % 2 == 0:
                    vX, vxr, j0 = vE, ver, cf // 2
                else:
                    vX, vxr, j0 = vO, vor, (cf - 1) // 2
                m0 = 7 - dl
                for hh in range(2):
                    h = 2 * hp + hh
                    po = 64 * hh
                    qsl = qT[po:po + 64, 64 * c:64 * c + 64]
                    pS, psr = pS_.next()
                    for j in range(4):
                        k0 = 64 * cf + 128 * j
                        P.mm(pS[:, 64 * j:64 * j + 64], kT[po:po + 64, k0:k0 + 128], qsl, True, True, [kr_, qr], [psr])
                    pm, pmr = pM_.next()
                    P.mm(pm[0:16, 0:64], kT[po:po + 64, 48:64], qsl, True, True, [kr_, qr], [pmr])
                    e32, e32r = E32_.next()
                    P.act(e32[:, :], pS[:, 0:256], AF.Exp, [psr], [e32r])
                    em, emr = EM_.next()
                    P.act(em[:, :], pm[0:16, 0:64], AF.Exp, [pmr], [emr])
                    eb, ebr = Eb_.next()
                    P.tt("vector", eb[:, :, :], e32[:, :].rearrange("p (k c) -> p k c", k=4), Wtab[:, h, m0:m0 + 7:2, :], ALU.mult,
                         [e32r, ("Wtab", h // 4)], [ebr])
                    po_, por = pO_.next()
                    for j in range(4):
                        P.mm(po_[:, 0:65], eb[:, j, :], vX[:, j0 + j, hh, :], j == 0, False, [ebr, vxr], [por])
                    P.mm(po_[:, 0:65], em[:, :], vM[:, hh, :], False, True, [emr, vmr], [por])
                    rc, rcr = rc_.next()
                    P.add("vector", (lambda e, a=rc, b=po_: e.reciprocal(a[:, :], b[:, 64:65])), [por], [rcr])
                    P.ts("vector", YP[:, c, 64 * hh:64 * hh + 64], po_[:, 0:64], rc[:, 0:1], ALU.mult, [por, rcr], [(ypr, c, hh)])
            pt, ptr = pT_.next()
            P.tr(pt[:, 0:16], YM[0:16, :], IDF[0:16, 0:16], [(ymr, 0), (ymr, 1), "IDF"], [ptr])
            P.copy("scalar", YT[:, 48:64], pt[:, 0:16], [ptr], [(ytr, "m")])
            for g in range(4):
                pt, ptr = pT_.next()
                for j in range(8):
                    c = 1 + 8 * g + j
                    P.tr(pt[:, 64 * j:64 * j + 64], YP[:, c, :], IDF[0:64, 0:64], [(ypr, c, 0), (ypr, c, 1), "IDF"], [ptr])
                P.copy("scalar", YT[:, 64 + 512 * g:64 + 512 * g + 512], pt[:, :], [ptr], [(ytr, g)])
            P.dma(T.XT_d[:, 8 + hp, :], YT[:, :], [(ytr, "pad"), (ytr, "m")] + [(ytr, g) for g in range(4)], [("XTd_n", hp)])
        P.emit()


def layer_norm_tail(P, z, zr, M, stats, mv, sm, LNW, LNB, smr):
    P.add("vector", (lambda e: e.bn_aggr(mv[0:M, :], stats[0:M, :, :])), [zr], [smr])
    P.act(sm[0:M, 0:1], mv[0:M, 1:2], AF.Ln, [smr], [smr], bias=EPS, scale=1.0)
    P.act(sm[0:M, 1:2], sm[0:M, 0:1], AF.Exp, [smr], [smr], scale=-0.5)
    P.stt("vector", sm[0:M, 2:3], mv[0:M, 0:1], -1.0, sm[0:M, 1:2], ALU.mult, ALU.mult, [smr], [smr])
    P.act(z[0:M, :], z[0:M, :], AF.Identity, [zr, smr], [zr], bias=sm[0:M, 2:3], scale=sm[0:M, 1:2])
    P.tt("gpsimd", z[0:M, :], z[0:M, :], LNW[0:M, :], ALU.mult, [zr, "LNW"], [zr])
    P.tt("gpsimd", z[0:M, :], z[0:M, :], LNB[0:M, :], ALU.add, [zr, "LNB"], [zr])


def phaseC(nc, P, T, l):
    with contextlib.ExitStack() as st:
        sb, ps, ring, pring = _alloc(nc, st, f"C{l}_")
        WO = sb("WO", [128, KC, D], BF16)
        WoV = T.w_out_b[l].rearrange("(kc k) c -> k kc c", k=128)
        for g in range(4):
            P.dma(WO[:, 4 * g:4 * g + 4, :], WoV[:, 4 * g:4 * g + 4, :], [], [("WO", g)])
        LNW = sb("LNW", [128, D], F32)
        LNB = sb("LNB", [128, D], F32)
        IDB = sb("IDB", [128, 128], BF16)
        P.dma(LNW[:, :], T.ln1w[l], [], ["LNW"])
        P.dma(LNB[:, :], T.ln1b[l], [], ["LNB"])
        P.dma(IDB[:, :], T.ident, [], ["IDB"], q="gpsimd")
        mt_ = ring("cmt", 2, [128, KC, 128], BF16)
        hr_ = ring("chr", 2, [128, D], F32)
        z_ = ring("cz", 2, [128, D], F32)
        zb_ = ring("czb", 2, [128, D], BF16)
        stg_ = ring("cstg", 2, [128, KC, 128], BF16)
        stats_ = ring("cstat", 2, [128, 4, 6], F32)
        mv_ = ring("cmv", 2, [128, 2], F32)
        sm_ = ring("csm", 2, [128, 4], F32)
        pz_ = pring("cpz", 6, [128, 512])
        pT_ = pring("cpT", 1, [128, KC, 128], BF16)
        for i in range(17):
            M = 128 if i < 16 else 64
            mt, mr = mt_.next()
            P.dma(mt[:, :, 0:M], T.XT_d[:, :, 128 * i:128 * i + M], [("XTd", i)], [mr])
            hr, hrr = hr_.next()
            P.dma(hr[0:M, :], T.Hres[128 * i:128 * i + M, :], [("Hres", i)], [hrr])
            z, zr = z_.next()
            stats, sr_ = stats_.next()
            for cb in range(4):
                pz, pzr = pz_.next()
                for kc in range(KC):
                    P.mm(pz[0:M, :], mt[:, kc, 0:M], WO[:, kc, 512 * cb:512 * cb + 512], kc == 0, kc == KC - 1, [mr, ("WO", kc // 4)], [pzr])
                zs = z[0:M, 512 * cb:512 * cb + 512]
                P.stt("vector", zs, hr[0:M, 512 * cb:512 * cb + 512], ALPHA, pz[0:M, :], ALU.mult, ALU.add, [hrr, pzr], [zr])
                P.add("vector", (lambda e, a=stats, b=zs, cb=cb, M=M: e.bn_stats(a[0:M, cb, :], b)), [zr], [zr])
            mv, mvr = mv_.next()
            sm, smr = sm_.next()
            layer_norm_tail(P, z, zr, M, stats, mv, sm, LNW, LNB, smr)
            P.dma(T.Hres[128 * i:128 * i + M, :], z[0:M, :], [zr], [("Hres", i)])
            zb, zbr = zb_.next()
            P.copy("scalar", zb[0:M, :], z[0:M, :], [zr], [zbr])
            pT, pTr = pT_.next()
            for kc in range(KC):
                P.tr(pT[:, kc, 0:M], zb[0:M, 128 * kc:128 * kc + 128], IDB[0:M, 0:M], [zbr, "IDB"], [pTr])
            sg, sgr = stg_.next()
            P.copy("vector", sg[:, :, 0:M], pT[:, :, 0:M], [pTr], [sgr])
            P.dma(T.XT_d[:, :, 128 * i:128 * i + M], sg[:, :, 0:M], [sgr], [("XTd", i)])
        P.emit()


def phaseD(nc, P, T, l, last):
    with contextlib.ExitStack() as st:
        sb, ps, ring, pring = _alloc(nc, st, f"D{l}_")
        LNW = sb("LNW", [128, D], F32)
        LNB = sb("LNB", [128, D], F32)
        IDB = sb("IDB", [128, 128], BF16)
        IDF = sb("IDF", [128, 128], F32)
        P.dma(LNW[:, :], T.ln2w[l], [], ["LNW"])
        P.dma(LNB[:, :], T.ln2b[l], [], ["LNB"])
        P.dma(IDB[:, :], T.ident, [], ["IDB"], q="gpsimd")
        P.dma(IDF[:, :], T.ident, [], ["IDF"])
        NT = 448
        xt_ = ring("dxt", 1, [128, KC, NT], BF16)
        uT = sb("duT", [128, 64, NT], BF16)
        w1_ = ring("dw1", 2, [128, KC, 256], BF16)
        w2_ = ring("dw2", 3, [128, 32, 128], BF16)
        Y2T = sb("dY2T", [128, KC, NT], F32)
        rl_ = ring("drl", 2, [128, NT], F32)
        hr_ = ring("dhr", 1, [128, D], F32)
        z_ = ring("dz", 2, [128, D], F32)
        zb_ = ring("dzb", 1, [128, D], BF16)
        stg_ = ring("dstg", 1, [128, KC, 128], BF16)
        stats_ = ring("dstat", 2, [128, 4, 6], F32)
        mv_ = ring("dmv", 2, [128, 2], F32)
        sm_ = ring("dsm", 2, [128, 4], F32)
        pm_ = pring("dpm", 3, [128, 512])
        pz_ = pring("dpz", 3, [128, 512])
        pT_ = pring("dpT", 1, [128, KC, 128], BF16)
        W1v = T.w_ff1_b[l].rearrange("(kc k) f -> k kc f", k=128)
        W2v = T.w_ff2_b[l].rearrange("(fc f) d -> f fc d", f=128)
        TGS = [(0, 448), (448, 448), (896, 448), (1344, 384), (1728, 384)]
        ev = 0
        for (t0, n) in TGS:
            xt, xr = xt_.next()
            P.dma(xt[:, :, 0:n], T.XT_d[:, :, t0:t0 + n], [("XTd", t0 + 128 * s) for s in range((n + 127) // 128)], [xr])
            for fb2 in range(32):
                w1, w1r = w1_.next()
                P.dma(w1[:, :, :], W1v[:, :, 256 * fb2:256 * fb2 + 256], [], [w1r])
                for j in range(2):
                    fb = 2 * fb2 + j
                    pm, pmr = pm_.next()
                    for kc in range(KC):
                        P.mm(pm[:, 0:n], w1[:, kc, 128 * j:128 * j + 128], xt[:, kc, 0:n], kc == 0, kc == KC - 1, [w1r, xr], [pmr])
                    rl, rlr = rl_.next()
                    P.act(rl[:, 0:n], pm[:, 0:n], AF.Relu, [pmr], [rlr])
                    P.tt("vector" if ev % 2 == 0 else "gpsimd", uT[:, fb, 0:n], rl[:, 0:n], rl[:, 0:n], ALU.mult, [rlr], [("uT", fb)])
                    ev += 1
            for db in range(16):
                pm, pmr = pm_.next()
                for half in range(2):
                    w2, w2r = w2_.next()
                    P.dma(w2[:, :, :], W2v[:, 32 * half:32 * half + 32, 128 * db:128 * db + 128], [], [w2r])
                    for f in range(32):
                        fc = 32 * half + f
                        P.mm(pm[:, 0:n], w2[:, f, :], uT[:, fc, 0:n], fc == 0, fc == 63, [w2r, ("uT", fc)], [pmr])
                if db % 2 == 0:
                    P.copy("scalar", Y2T[:, db, 0:n], pm[:, 0:n], [pmr], [("Y2T", db)])
                else:
                    P.copy("vector", Y2T[:, db, 0:n], pm[:, 0:n], [pmr], [("Y2T", db)])
            for s in range((n + 127) // 128):
                m = min(128, n - 128 * s)
                p0 = t0 + 128 * s
                hr, hrr = hr_.next()
                P.dma(hr[0:m, :], T.Hres[p0:p0 + m, :], [("Hres", p0)], [hrr])
                z, zr = z_.next()
                stats, sr_ = stats_.next()
                for cb in range(4):
                    pz, pzr = pz_.next()
                    for dj in range(4):
                        db = 4 * cb + dj
                        P.tr(pz[0:m, 128 * dj:128 * dj + 128], Y2T[:, db, 128 * s:128 * s + m], IDF[:, :], [("Y2T", db), "IDF"], [pzr])
                    zs = z[0:m, 512 * cb:512 * cb + 512]
                    P.stt("vector", zs, hr[0:m, 512 * cb:512 * cb + 512], ALPHA, pz[0:m, :], ALU.mult, ALU.add, [hrr, pzr], [zr])
                    P.add("vector", (lambda e, a=stats, b=zs, cb=cb, m=m: e.bn_stats(a[0:m, cb, :], b)), [zr], [zr])
                mv, mvr = mv_.next()
                sm, smr = sm_.next()
                layer_norm_tail(P, z, zr, m, stats, mv, sm, LNW, LNB, smr)
                if last:
                    r0 = max(p0, 64)
                    if r0 < p0 + m:
                        P.dma(T.out[r0 - 64:p0 + m - 64, :], z[r0 - p0:m, :], [zr], [("out", p0)])
                    continue
                P.dma(T.Hres[p0:p0 + m, :], z[0:m, :], [zr], [("Hres", p0)])
                zb, zbr = zb_.next()
                P.copy("scalar", zb[0:m, :], z[0:m, :], [zr], [zbr])
                pT, pTr = pT_.next()
                for kc in range(KC):
                    P.tr(pT[:, kc, 0:m], zb[0:m, 128 * kc:128 * kc + 128], IDB[0:m, 0:m], [zbr, "IDB"], [pTr])
                sg, sgr = stg_.next()
                P.copy("vector", sg[:, :, 0:m], pT[:, :, 0:m], [pTr], [sgr])
                if p0 == 0:
                    P.memset("vector", sg[:, :, 0:48], 0.0, [sgr], [sgr])
                P.dma(T.XT_d[:, :, p0:p0 + m], sg[:, :, 0:m], [sgr], [("XTd", p0)])
        P.emit()


def build_program(upto=None, dbg=(), wl=DEPTH, ffl=DEPTH):
    nc = bass.Bass("TRN2", target_bir_lowering=False)
    T = Ctx()

    def din(name, shape, dt=F32):
        return nc.dram_tensor(name, list(shape), dt, kind="ExternalInput").ap()

    def dscr(name, shape, dt):
        kind = "ExternalOutput" if name in dbg else "Internal"
        return nc.dram_tensor(name, list(shape), dt, kind=kind).ap()

    T.x = din("x", [2048, D])
    T.meta = din("meta", [16, D])
    T.wl, T.ffl = wl, ffl
    T.w_in = din("w_in", [wl, D, PT])
    T.w_out = din("w_out", [wl, D, D])
    T.w_ff1 = din("w_ff1", [ffl, D, DFF]) if ffl else None
    T.w_ff2 = din("w_ff2", [ffl, DFF, D]) if ffl else None
    T.wup = din("wup", [DEPTH, 2, 17, 512])
    T.gnw = din("gnw", [DEPTH, 128, 512])
    T.bt = din("bt", [DEPTH, 128, 16, 14, 64])
    T.nmask = din("nmask", [128, 64])
    T.cst = din("cst", [64, 6, 64])
    T.ident = din("ident", [128, 128])
    T.ln1w = din("ln1w", [DEPTH, 128, D])
    T.ln1b = din("ln1b", [DEPTH, 128, D])
    T.ln2w = din("ln2w", [DEPTH, 128, D])
    T.ln2b = din("ln2b", [DEPTH, 128, D])
    T.out = nc.dram_tensor("out", [2048, D], F32, kind="ExternalOutput").ap()
    T.Hres = dscr("Hres", [LP, D], F32)
    T.XT_d = dscr("XT_d", [128, KC, LP], BF16)
    T.w_in_b = [dscr(f"w_in_b{l}", [D, PT], BF16) for l in range(DEPTH)]
    T.w_out_b = [dscr(f"w_out_b{l}", [D, D], BF16) for l in range(DEPTH)]
    T.w_ff1_b = [dscr(f"w_ff1_b{l}", [D, DFF], BF16) for l in range(DEPTH)]
    T.w_ff2_b = [dscr(f"w_ff2_b{l}", [DFF, D], BF16) for l in range(DEPTH)]
    T.QK_d = dscr("QK_d", [24, 128, LP], BF16)
    T.LRT_d = dscr("LRT_d", [2, 16, LP], F32)
    T.TMK = dscr("TMK", [LP, 512], BF16)
    T.TMV = dscr("TMV", [LP, 1024], BF16)
    T.TMR = dscr("TMR", [LP, 1024], F32)
    T.TMNV = dscr("TMNV", [LP, 1024], BF16)
    T.OF_d = dscr("OF_d", [LP, 1024], F32)

    phases = [("0", lambda P: phase0(nc, P, T))]
    for l in range(DEPTH):
        phases.append((f"A{l}", lambda P, l=l: phaseA(nc, P, T, l)))
        phases.append((f"G{l}", lambda P, l=l: phaseG(nc, P, T, l)))
        phases.append((f"N{l}", lambda P, l=l: phaseN(nc, P, T, l)))
        phases.append((f"C{l}", lambda P, l=l: phaseC(nc, P, T, l)))
        phases.append((f"D{l}", lambda P, l=l: phaseD(nc, P, T, l, l == DEPTH - 1)))
    with contextlib.ExitStack() as st:
        P = Prog(nc, st)
        for name, fn in phases:
            fn(P)
            if upto is not None and name == upto:
                break
    return nc


def host_inputs(inputs):
    f32 = np.float32
    s = np.arange(64)
    le = (s[:, None] <= s[None, :]).astype(f32)
    ge = (s[:, None] >= s[None, :]).astype(f32)
    gt = (s[:, None] > s[None, :]).astype(f32)
    lt = (s[:, None] < s[None, :]).astype(f32)
    g = f32(-1.0 / 16.0)
    cst = np.stack([le * g, ge * g, gt * g, lt * g, le, ge], axis=1).astype(f32)
    kc = np.arange(64)[:, None]
    c = np.arange(64)[None, :]
    cs = np.clip(c - 8, 0, 48)
    nmask = ((kc >= cs) & (kc < cs + 16)).astype(f32)
    dc = np.clip(kc - c, -15, 15) + 15
    rb = np.asarray(inputs["na_rel_bias"], f32)
    bt = rb[:, :, :, dc]
    bt = bt.transpose(0, 3, 1, 2, 4)
    bt = np.ascontiguousarray(np.concatenate([bt[:, :, :, 0:14, :], bt[:, :, :, 1:15, :]], axis=1))
    nmask = np.ascontiguousarray(np.concatenate([nmask, nmask], axis=0))
    wup = np.concatenate([np.asarray(inputs["gla_w_up"], f32), np.asarray(inputs["gla_b_up"], f32)[:, :, None, :]], axis=2)
    gnw = np.ascontiguousarray(np.broadcast_to(np.tile(np.asarray(inputs["gla_norm_w"], f32), (1, 2))[:, None, :], (DEPTH, 128, 512)))
    rep = lambda a: np.ascontiguousarray(np.broadcast_to(np.asarray(a, f32)[:, None, :], (DEPTH, 128, D)))
    shared = {
        "meta": np.asarray(inputs["meta"], f32),
        "w_in": np.asarray(inputs["w_in"], f32),
        "w_out": np.asarray(inputs["w_out"], f32),
        "w_ff1": np.asarray(inputs["w_ff1"], f32),
        "w_ff2": np.asarray(inputs["w_ff2"], f32),
        "wup": np.ascontiguousarray(wup),
        "gnw": gnw,
        "bt": bt,
        "nmask": nmask,
        "cst": cst,
        "ident": np.eye(128, dtype=f32),
        "ln1w": rep(inputs["ln1_w"]),
        "ln1b": rep(inputs["ln1_b"]),
        "ln2w": rep(inputs["ln2_w"]),
        "ln2b": rep(inputs["ln2_b"]),
    }
    return shared


def kernel(**inputs):
    x = np.asarray(inputs["x"], np.float32)
    shared = host_inputs(inputs)
    nc = build_program()
    in_maps = []
    for b in range(NCORES):
        m = dict(shared)
        m["x"] = np.ascontiguousarray(x[b])
        in_maps.append(m)
    res = run_bass_kernel_spmd(nc, in_maps, core_ids=list(range(NCORES)))
    return np.stack([np.asarray(r["out"], np.float32) for r in res.results], axis=0)
```
